# Optimizing a Trainium2 kernel written in Bass

```python
import jax, jax.numpy as jnp
from jax import lax
import numpy as np

D_MODEL = 1024
BATCH = 4
SEQ = 8192
DEPTH = 2

PLE_DIM = 256
D_MIX = D_MODEL
MLA_HEADS = 8
MLA_NOPE = 64
MLA_ROPE = 32
MLA_V = 64
Q_LORA = 256
KV_LORA = 128
DIL_HEADS = 8
DIL_HEAD_DIM = 64
DIL_CONFIGS = ((128, 1), (512, 4), (2048, 16))
D_FF = 2816
MACARON = 0.5
ROPE_THETA = 10000.0
EPS = 1e-6
Q_BLOCK = 128
NEG = -1e30
N_NORMS = 8

MLA_WIDTH = MLA_HEADS * MLA_V
DIL_WIDTH = DIL_HEADS * DIL_HEAD_DIM
N_IN = Q_LORA + KV_LORA + MLA_ROPE + 3 * DIL_WIDTH

kernel_name = "hybrid_mla_dilated_macaron_block"


def rmsnorm(x, g):
    xf = x.astype(jnp.float32)
    y = xf * lax.rsqrt(jnp.mean(xf * xf, axis=-1, keepdims=True) + EPS)
    return (y * g.astype(jnp.float32)).astype(x.dtype)


def rope_tables(positions, dim, dtype):
    inv = ROPE_THETA ** (-jnp.arange(0, dim, 2, dtype=jnp.float32) / dim)
    ang = positions.astype(jnp.float32)[..., None] * inv
    return jnp.cos(ang)[:, :, None, :].astype(dtype), jnp.sin(ang)[:, :, None, :].astype(dtype)


def apply_rope(x, cos, sin):
    x1, x2 = jnp.split(x, 2, axis=-1)
    return jnp.concatenate([x1 * cos - x2 * sin, x2 * cos + x1 * sin], axis=-1)


def swiglu(x, w_gate, w_up, w_down):
    return (jax.nn.silu(x @ w_gate) * (x @ w_up)) @ w_down


def mla_attention(q_nope, q_rope, k_nope, k_rope, v):
    B, S, H, _ = q_nope.shape
    nb = S // Q_BLOCK
    scale = (MLA_NOPE + MLA_ROPE) ** -0.5
    key_pos = jnp.arange(S)

    def block(n):
        start = n * Q_BLOCK
        qn = lax.dynamic_slice_in_dim(q_nope, start, Q_BLOCK, axis=1)
        qr = lax.dynamic_slice_in_dim(q_rope, start, Q_BLOCK, axis=1)
        s = (jnp.einsum('bqhd,bkhd->bhqk', qn, k_nope)
             + jnp.einsum('bqhd,bkd->bhqk', qr, k_rope)).astype(jnp.float32) * scale
        q_pos = start + jnp.arange(Q_BLOCK)
        causal = key_pos[None, :] <= q_pos[:, None]
        s = jnp.where(causal[None, None], s, NEG)
        w = jax.nn.softmax(s, axis=-1).astype(v.dtype)
        return jnp.einsum('bhqk,bkhd->bqhd', w, v)

    out = lax.map(block, jnp.arange(nb))
    return out.transpose(1, 0, 2, 3, 4).reshape(B, S, H * v.shape[-1])


def dilated_branch(q, k, v, window, dilation):
    B, S, H, dh = q.shape
    L = S // dilation
    w_sub = window // dilation
    nb = -(-L // Q_BLOCK)
    Lp = nb * Q_BLOCK

    def to_sub(t):
        t = t.reshape(B, L, dilation, H, dh).transpose(0, 2, 1, 3, 4).reshape(B * dilation, L, H, dh)
        t = jnp.pad(t, ((0, 0), (0, Lp - L), (0, 0), (0, 0)))
        return t.reshape(B * dilation, nb, Q_BLOCK, H, dh)

    def with_prev(t):
        prev = jnp.pad(t, ((0, 0), (1, 0), (0, 0), (0, 0), (0, 0)))[:, :-1]
        return jnp.concatenate([prev, t], axis=2)

    qs, ks, vs = to_sub(q), to_sub(k), to_sub(v)
    kk, vv = with_prev(ks), with_prev(vs)
    s = jnp.einsum('gnqhd,gnkhd->gnhqk', qs, kk).astype(jnp.float32) * (dh ** -0.5)
    qi = jnp.arange(Q_BLOCK)[:, None]
    ki = jnp.arange(2 * Q_BLOCK)[None, :]
    dist = qi + Q_BLOCK - ki
    key_sub = jnp.arange(nb)[:, None, None] * Q_BLOCK + ki[None] - Q_BLOCK
    valid = (dist >= 0)[None] & (dist <= w_sub)[None] & (key_sub >= 0)
    s = jnp.where(valid[None, :, None], s, NEG)
    m = jnp.max(s, axis=-1, keepdims=True)
    e = jnp.exp(s - m)
    den = jnp.sum(e, axis=-1, keepdims=True)
    lse = (m + jnp.log(den))[..., 0]
    o = jnp.einsum('gnhqk,gnkhd->gnqhd', (e / den).astype(v.dtype), vv)
    o = (o.reshape(B * dilation, Lp, H, dh)[:, :L]
         .reshape(B, dilation, L, H, dh).transpose(0, 2, 1, 3, 4).reshape(B, S, H, dh))
    lse = (lse.transpose(0, 1, 3, 2).reshape(B * dilation, Lp, H)[:, :L]
           .reshape(B, dilation, L, H).transpose(0, 2, 1, 3).reshape(B, S, H))
    return o, lse


def dilated_attention(q, k, v):
    outs, lses = [], []
    for window, dilation in DIL_CONFIGS:
        o, l = dilated_branch(q, k, v, window, dilation)
        outs.append(o)
        lses.append(l)
    wts = jax.nn.softmax(jnp.stack(lses, axis=0), axis=0).astype(q.dtype)
    return jnp.einsum('kbsh,kbshd->bshd', wts, jnp.stack(outs, axis=0))


def hybrid_mixer(h, cos_m, sin_m, cos_d, sin_d, w_in, q_norm, w_q_up, kv_norm, w_kv_up, grp_gain, w_out):
    B, S, _ = h.shape
    z = h @ w_in
    c0 = Q_LORA
    c1 = c0 + KV_LORA
    c2 = c1 + MLA_ROPE
    q_lat, kv_lat, k_rope, qkv_d = jnp.split(z, [c0, c1, c2], axis=-1)
    q = (rmsnorm(q_lat, q_norm) @ w_q_up).reshape(B, S, MLA_HEADS, MLA_NOPE + MLA_ROPE)
    q_nope = q[..., :MLA_NOPE]
    q_rope = apply_rope(q[..., MLA_NOPE:], cos_m, sin_m)
    kv = (rmsnorm(kv_lat, kv_norm) @ w_kv_up).reshape(B, S, MLA_HEADS, MLA_NOPE + MLA_V)
    k_nope, v_mla = kv[..., :MLA_NOPE], kv[..., MLA_NOPE:]
    k_rope = apply_rope(k_rope[:, :, None, :], cos_m, sin_m)[:, :, 0]
    o_mla = mla_attention(q_nope, q_rope, k_nope, k_rope, v_mla)
    qd, kd, vd = [t.reshape(B, S, DIL_HEADS, DIL_HEAD_DIM) for t in jnp.split(qkv_d, 3, axis=-1)]
    qd = apply_rope(qd, cos_d, sin_d)
    kd = apply_rope(kd, cos_d, sin_d)
    o_dil = dilated_attention(qd, kd, vd).reshape(B, S, DIL_WIDTH)
    merged = jnp.concatenate([rmsnorm(o_mla, grp_gain[:MLA_WIDTH]),
                              rmsnorm(o_dil, grp_gain[MLA_WIDTH:])], axis=-1)
    return merged @ w_out


def setup_inputs(seed: int = 0) -> dict:
    key = jax.random.key(seed)
    ks = jax.random.split(key, 16)
    f32 = jnp.float32

    def w(k, shape, fan_in):
        return jax.random.normal(k, shape, f32) * (fan_in ** -0.5)

    def gain(k, shape):
        return 1.0 + 0.02 * jax.random.normal(k, shape, f32)

    return {
        "x": jax.random.normal(ks[0], (BATCH, SEQ, D_MODEL), f32),
        "p": jax.random.normal(ks[1], (DEPTH, BATCH, SEQ, PLE_DIM), f32),
        "positions": jnp.broadcast_to(jnp.arange(SEQ, dtype=jnp.int32), (BATCH, SEQ)),
        "norm_gains": gain(ks[2], (DEPTH, N_NORMS, D_MODEL)),
        "w_in": w(ks[3], (DEPTH, D_MODEL, N_IN), D_MODEL),
        "q_norm": gain(ks[4], (DEPTH, Q_LORA)),
        "w_q_up": w(ks[5], (DEPTH, Q_LORA, MLA_HEADS * (MLA_NOPE + MLA_ROPE)), Q_LORA),
        "kv_norm": gain(ks[6], (DEPTH, KV_LORA)),
        "w_kv_up": w(ks[7], (DEPTH, KV_LORA, MLA_HEADS * (MLA_NOPE + MLA_V)), KV_LORA),
        "group_out_norm": gain(ks[8], (DEPTH, D_MIX)),
        "w_out": w(ks[9], (DEPTH, D_MIX, D_MODEL), D_MIX),
        "ffn_gate": w(ks[10], (DEPTH, 2, D_MODEL, D_FF), D_MODEL),
        "ffn_up": w(ks[11], (DEPTH, 2, D_MODEL, D_FF), D_MODEL),
        "ffn_down": w(ks[12], (DEPTH, 2, D_FF, D_MODEL), D_FF),
        "w_ple": w(ks[13], (DEPTH, PLE_DIM, D_MODEL), PLE_DIM),
        "w_ple_gate": w(ks[14], (DEPTH, D_MODEL, D_MODEL), D_MODEL),
    }


def reference(x, p, positions, norm_gains, w_in, q_norm, w_q_up, kv_norm, w_kv_up,
              group_out_norm, w_out, ffn_gate, ffn_up, ffn_down, w_ple, w_ple_gate):
    cos_m, sin_m = rope_tables(positions, MLA_ROPE, x.dtype)
    cos_d, sin_d = rope_tables(positions, DIL_HEAD_DIM, x.dtype)
    h = x
    for i in range(DEPTH):
        g = norm_gains[i]
        f = swiglu(rmsnorm(h, g[0]), ffn_gate[i, 0], ffn_up[i, 0], ffn_down[i, 0])
        h = h + MACARON * rmsnorm(f, g[1])
        mix = hybrid_mixer(rmsnorm(h, g[2]), cos_m, sin_m, cos_d, sin_d, w_in[i], q_norm[i],
                           w_q_up[i], kv_norm[i], w_kv_up[i], group_out_norm[i], w_out[i])
        h = h + rmsnorm(mix, g[3])
        f = swiglu(rmsnorm(h, g[4]), ffn_gate[i, 1], ffn_up[i, 1], ffn_down[i, 1])
        h = h + MACARON * rmsnorm(f, g[5])
        gate = jax.nn.sigmoid(rmsnorm(h, g[6]) @ w_ple_gate[i])
        h = h + rmsnorm((p[i].astype(h.dtype) @ w_ple[i]) * gate, g[7])
    return h
```

```python
import numpy as np
import concourse.bass as bass
import concourse.mybir as mybir

F32 = mybir.dt.float32
BF16 = mybir.dt.bfloat16
I32 = mybir.dt.int32
AF = mybir.ActivationFunctionType
ALU = mybir.AluOpType

SEM_LIMIT = 30000


class _Op:
    __slots__ = ("eng", "fn", "deps", "is_dma", "semkey", "sem", "count", "signal", "idx")


class Prog:
    ENG = ("pe", "act", "dve", "pool", "sp")

    def __init__(self, nc):
        self.nc = nc
        self.ops = []
        self.res = {}

    def op(self, eng, fn, reads=(), writes=(), dma=None):
        o = _Op()
        o.eng = eng
        o.fn = fn
        o.is_dma = dma is not None
        o.semkey = ("dma", dma) if o.is_dma else ("eng", eng)
        o.signal = False
        o.idx = len(self.ops)
        deps = {}
        res = self.res
        for r in reads:
            e = res.get(r)
            if e is not None and e[0] is not None:
                deps[e[0].idx] = e[0]
        for w in writes:
            e = res.get(w)
            if e is not None:
                if e[0] is not None:
                    deps[e[0].idx] = e[0]
                for rd in e[1]:
                    deps[rd.idx] = rd
        for r in reads:
            e = res.get(r)
            if e is None:
                res[r] = [None, [o]]
            else:
                e[1].append(o)
        for w in writes:
            res[w] = [o, []]
        o.deps = [d for d in deps.values() if d is not o]
        for d in o.deps:
            d.signal = True
        self.ops.append(o)
        return o

    def emit(self, final_wait_ops=(), max_ops=None):
        nc = self.nc
        if max_ops is not None:
            self.ops = self.ops[:max_ops]
            final_wait_ops = [o for o in self.ops if o.is_dma]
        import contextlib
        stack = contextlib.ExitStack()
        sems = {}
        counts = {}
        nsem = [0]

        def new_sem():
            nsem[0] += 1
            return stack.enter_context(nc.semaphore(f"s{nsem[0]}"))

        for o in self.ops:
            if o.is_dma:
                o.signal = True
            if not o.signal:
                continue
            k = o.semkey
            step = 16 if o.is_dma else 1
            if k not in sems or counts[k] + step > SEM_LIMIT:
                sems[k] = new_sem()
                counts[k] = 0
            counts[k] += step
            o.sem = sems[k]
            o.count = counts[k]
        per_eng = {e: [] for e in self.ENG}
        for o in self.ops:
            per_eng[o.eng].append(o)
        engobj = {"pe": "tensor", "act": "scalar", "dve": "vector", "pool": "gpsimd", "sp": "sync"}
        finals = list(final_wait_ops)
        with stack:
            with nc.Block() as block:
                for ename in self.ENG:
                    ops = per_eng[ename]

                    def body(e, ops=ops, ename=ename):
                        waited = {}
                        for o in ops:
                            need = {}
                            for d in o.deps:
                                key = id(d.sem)
                                if waited.get(key, 0) >= d.count:
                                    continue
                                if key not in need or need[key][1] < d.count:
                                    need[key] = (d.sem, d.count)
                            for key, (s, c) in need.items():
                                e.wait_ge(s, c)
                                waited[key] = c
                            ins = o.fn(e)
                            if o.signal:
                                ins.then_inc(o.sem, 16 if o.is_dma else 1)
                        if ename == "sp":
                            need = {}
                            for d in finals:
                                key = id(d.sem)
                                if key not in need or need[key][1] < d.count:
                                    need[key] = (d.sem, d.count)
                            for key, (s, c) in need.items():
                                e.wait_ge(s, c)

                    getattr(block, engobj[ename])(body)
        return nsem[0]


from contextlib import ExitStack

D_MODEL = 1024
D_FF = 2816
FC = 22
EPS = 1e-6
TW = 1024
NB = TW // 512
NTOK = 4096


def prep_w(W):
    K, N = W.shape
    return np.ascontiguousarray(W.reshape(K // 128, 128, N // 128, 128).transpose(2, 1, 0, 3))


def prep_vec(v):
    sh = v.shape
    C = sh[-1] // 128
    v2 = v.reshape(-1, C, 128)
    return np.ascontiguousarray(v2.transpose(2, 0, 1))


class RL:
    def __init__(self, nc, P, es, nvec):
        self.nc, self.P = nc, P
        sb = lambda name, shape, dt: es.enter_context(nc.sbuf_tensor("sb_" + name, shape, dt))
        self.h = sb("h", [128, 8, TW], F32)
        self.hn = sb("hn", [128, 8, TW], BF16)
        self.act = sb("act", [128, FC, TW], BF16)
        self.fbuf = sb("fbuf", [128, 8, TW], F32)
        self.sqb = sb("sqb", [128, 4, 512], BF16)
        self.rstd = sb("rstd", [128, 2, TW], F32)
        self.sg = sb("sg", [128, 2, 512], BF16)
        self.tmp = sb("tmp", [128, 2, 512], F32)
        self.wbuf = sb("wbuf", [128, 6, FC * 128], BF16)
        self.gains = sb("gains", [128, nvec, 8], F32)
        self.ones = sb("ones", [128, 128], BF16)
        self.pbf = sb("pbf", [128, 2, TW], BF16)
        self.ps = [es.enter_context(nc.psum_tensor(f"ps{i}", [128, 512], F32)) for i in range(8)]
        self.wslot = 0
        self.bank = 0
        self.sqslot = 0
        self.sgslot = 0
        self.tmpslot = 0

    def setup(self, gains_dram):
        P = self.P
        P.op("pool", lambda e: e.memset(self.ones[:], 1.0), writes=["ones"])
        P.op("sp", lambda e: e.dma_start(out=self.gains[:], in_=gains_dram), writes=["gains"], dma="gains")

    def next_bank(self):
        b = self.bank
        self.bank = (self.bank + 1) % 6
        return b

    def stats(self, src_fn, src_res, blk, first, last, bank):
        P = self.P
        s = self.sqslot
        self.sqslot = (s + 1) % 4
        sq = self.sqb[:, s, :]
        P.op("act", lambda e: e.activation(out=sq, in_=src_fn(), func=AF.Square),
             reads=src_res, writes=[("sqb", s)])
        P.op("pe", lambda e: e.matmul(self.ps[bank][:], lhsT=self.ones[:], rhs=sq, start=first, stop=last),
             reads=[("sqb", s), "ones"], writes=[("ps", bank)])

    def rstd_from(self, bank, slot, blk, n, scale=1.0):
        P = self.P
        r = self.rstd[:, slot, blk * 512:(blk + 1) * 512]
        s2 = 1.0 / (scale * scale)
        P.op("act", lambda e: e.activation(out=r, in_=self.ps[bank][:], func=AF.Sqrt, scale=s2 / n, bias=EPS * s2),
             reads=[("ps", bank)], writes=[("rstd", slot, blk)])
        P.op("dve", lambda e: e.reciprocal(out=r, in_=r), reads=[("rstd", slot, blk)], writes=[("rstd", slot, blk)])

    def prenorm(self, gi, src="h"):
        P = self.P
        srcT = self.h if src == "h" else self.fbuf
        for blk in range(NB):
            cs = slice(blk * 512, (blk + 1) * 512)
            bank = 6 + blk % 2
            for c in range(8):
                self.stats(lambda c=c, cs=cs: srcT[:, c, cs], [(src, c, blk)], blk, c == 0, c == 7, bank)
            self.rstd_from(bank, 0, blk, D_MODEL)
            for c in range(8):
                P.op("dve", lambda e, c=c, cs=cs: e.scalar_tensor_tensor(
                    out=self.hn[:, c, cs], in0=srcT[:, c, cs], scalar=self.gains[:, gi, c:c + 1],
                    in1=self.rstd[:, 0, cs], op0=ALU.mult, op1=ALU.mult),
                    reads=[(src, c, blk), ("rstd", 0, blk), "gains"], writes=[("hn", c, blk)])

    def postnorm_add(self, gi, scale):
        P = self.P
        for blk in range(NB):
            cs = slice(blk * 512, (blk + 1) * 512)
            bank = 6 + blk % 2
            self.rstd_from(bank, 1, blk, D_MODEL, scale)
            for c in range(8):
                ts = self.tmpslot
                self.tmpslot = (ts + 1) % 2
                P.op("dve", lambda e, c=c, ts=ts, cs=cs: e.scalar_tensor_tensor(
                    out=self.tmp[:, ts, :], in0=self.fbuf[:, c, cs], scalar=self.gains[:, gi, c:c + 1],
                    in1=self.rstd[:, 1, cs], op0=ALU.mult, op1=ALU.mult),
                    reads=[("fbuf", c, blk), ("rstd", 1, blk), "gains"], writes=[("tmp", ts)])
                P.op("pool", lambda e, c=c, ts=ts, cs=cs: e.tensor_tensor(
                    out=self.h[:, c, cs], in0=self.h[:, c, cs], in1=self.tmp[:, ts, :], op=ALU.add),
                    reads=[("h", c, blk), ("tmp", ts)], writes=[("h", c, blk)])

    def linear(self, w_tile_fn, KC, n_oc, rhs_fn, rhs_res_fn, evac, M=128):
        P = self.P
        for oc in range(n_oc):
            ws = self.wslot
            self.wslot = (ws + 1) % 6
            wt = self.wbuf[:, ws, 0:KC * M].rearrange("p (k m) -> p k m", m=M)
            wflat = self.wbuf[:, ws, 0:KC * M]
            P.op("pool", lambda e, oc=oc, wflat=wflat: e.dma_start(out=wflat, in_=w_tile_fn(oc).rearrange("p k m -> p (k m)")),
                 writes=[("w", ws)], dma=f"w{ws}")
            for blk in range(NB):
                bank = self.next_bank()

                def mm(e, wt=wt, blk=blk, bank=bank):
                    ins = None
                    for kc in range(KC):
                        ins = e.matmul(self.ps[bank][0:M, :], lhsT=wt[:, kc, :], rhs=rhs_fn(kc, blk),
                                       start=(kc == 0), stop=(kc == KC - 1))
                    return ins
                rr = [("w", ws)]
                for kc in range(KC):
                    rr += rhs_res_fn(kc, blk)
                P.op("pe", mm, reads=rr, writes=[("ps", bank)])
                evac(oc, blk, bank)

    def evac_to_fbuf(self, oc, blk, bank):
        P = self.P
        cs = slice(blk * 512, (blk + 1) * 512)
        P.op("dve", lambda e: e.tensor_copy(out=self.fbuf[:, oc, cs], in_=self.ps[bank][:]),
             reads=[("ps", bank)], writes=[("fbuf", oc, blk)])
        self.stats(lambda: self.fbuf[:, oc, cs], [("fbuf", oc, blk)], blk, oc == 0, oc == 7, 6 + blk % 2)

    def ffn(self, wg, wu, wd, g_pre, g_post):
        P = self.P
        self.prenorm(g_pre)
        hn_fn = lambda kc, blk: self.hn[:, kc, blk * 512:(blk + 1) * 512]
        hn_res = lambda kc, blk: [("hn", kc, blk)]
        for fc in range(FC):
            def evac_gate(oc, blk, bank):
                s = self.sgslot
                self.sgslot = (s + 1) % 2
                self._sg_for[(oc, blk)] = s
                P.op("act", lambda e: e.activation(out=self.sg[:, s, :], in_=self.ps[bank][:], func=AF.Silu),
                     reads=[("ps", bank)], writes=[("sg", s)])

            def evac_up(oc, blk, bank, fc=fc):
                s = self._sg_for[(oc, blk)]
                cs = slice(blk * 512, (blk + 1) * 512)
                P.op("dve", lambda e: e.tensor_tensor(out=self.act[:, fc, cs], in0=self.ps[bank][:],
                                                      in1=self.sg[:, s, :], op=ALU.mult),
                     reads=[("ps", bank), ("sg", s)], writes=[("act", fc, blk)])
            self._sg_for = {}
            self.linear(lambda oc, fc=fc: wg[fc], 8, 1, hn_fn, hn_res, evac_gate)
            self.linear(lambda oc, fc=fc: wu[fc], 8, 1, hn_fn, hn_res, evac_up)
        act_fn = lambda kc, blk: self.act[:, kc, blk * 512:(blk + 1) * 512]
        act_res = lambda kc, blk: [("act", kc, blk)]
        self.linear(lambda oc: wd[oc], FC, 8, act_fn, act_res, self.evac_to_fbuf)
        self.postnorm_add(g_post, 0.5)

    def mixer_out(self, oT_tile, w_out, gg, g_post):
        P = self.P
        for c in range(8):
            P.op("sp", lambda e, c=c: e.dma_start(out=self.fbuf[:, c, :], in_=oT_tile(c)),
                 writes=[("fbuf", c, b) for b in range(NB)], dma=f"o{c}")
        for blk in range(NB):
            cs = slice(blk * 512, (blk + 1) * 512)
            for grp in range(2):
                bank = 6 + grp
                for c4 in range(4):
                    c = grp * 4 + c4
                    self.stats(lambda c=c, cs=cs: self.fbuf[:, c, cs], [("fbuf", c, blk)], blk, c4 == 0, c4 == 3, bank)
                self.rstd_from(bank, grp, blk, 512)
                for c4 in range(4):
                    c = grp * 4 + c4
                    P.op("dve", lambda e, c=c, grp=grp, cs=cs: e.scalar_tensor_tensor(
                        out=self.hn[:, c, cs], in0=self.fbuf[:, c, cs], scalar=self.gains[:, gg, c:c + 1],
                        in1=self.rstd[:, grp, cs], op0=ALU.mult, op1=ALU.mult),
                        reads=[("fbuf", c, blk), ("rstd", grp, blk), "gains"], writes=[("hn", c, blk)])
        hn_fn = lambda kc, blk: self.hn[:, kc, blk * 512:(blk + 1) * 512]
        hn_res = lambda kc, blk: [("hn", kc, blk)]
        self.linear(lambda oc: w_out[oc], 8, 8, hn_fn, hn_res, self.evac_to_fbuf)
        self.postnorm_add(g_post, 1.0)

    def ple(self, pT_tile, w_gate, w_ple, g_pre, g_post):
        P = self.P
        for c in range(2):
            P.op("pool", lambda e, c=c: e.dma_start(out=self.pbf[:, c, :], in_=pT_tile(c)), writes=["pbf"], dma=f"pbf{c}")
        self.prenorm(g_pre)
        hn_fn = lambda kc, blk: self.hn[:, kc, blk * 512:(blk + 1) * 512]
        hn_res = lambda kc, blk: [("hn", kc, blk)]

        def evac_gate(oc, blk, bank):
            cs = slice(blk * 512, (blk + 1) * 512)
            P.op("act", lambda e: e.activation(out=self.fbuf[:, oc, cs], in_=self.ps[bank][:], func=AF.Sigmoid),
                 reads=[("ps", bank)], writes=[("fbuf", oc, blk)])
        self.linear(lambda oc: w_gate[oc], 8, 8, hn_fn, hn_res, evac_gate)

        def evac_p(oc, blk, bank):
            cs = slice(blk * 512, (blk + 1) * 512)
            P.op("dve", lambda e: e.tensor_tensor(out=self.fbuf[:, oc, cs], in0=self.ps[bank][:],
                                                  in1=self.fbuf[:, oc, cs], op=ALU.mult),
                 reads=[("ps", bank), ("fbuf", oc, blk)], writes=[("fbuf", oc, blk)])
            self.stats(lambda: self.fbuf[:, oc, cs], [("fbuf", oc, blk)], blk, oc == 0, oc == 7, 6 + blk % 2)
        p_fn = lambda kc, blk: self.pbf[:, kc, blk * 512:(blk + 1) * 512]
        self.linear(lambda oc: w_ple[oc], 2, 8, p_fn, lambda kc, blk: ["pbf"], evac_p)
        self.postnorm_add(g_post, 1.0)

    def load_h(self, hT, t):
        P = self.P
        for c in range(8):
            P.op("sp", lambda e, c=c: e.dma_start(
                out=self.h[:, c, :], in_=hT[c * 128:(c + 1) * 128, t * TW:(t + 1) * TW]),
                writes=[("h", c, b) for b in range(NB)], dma=f"h{c}")

    def store_h(self, hT_out, t):
        P = self.P
        ops = []
        for c in range(8):
            ops.append(P.op("sp", lambda e, c=c: e.dma_start(
                out=hT_out[c * 128:(c + 1) * 128, t * TW:(t + 1) * TW], in_=self.h[:, c, :]),
                reads=[("h", c, b) for b in range(NB)], dma=f"hs{c}"))
        return ops

    def store_hn(self, hnT_out, t):
        P = self.P
        ops = []
        for c in range(8):
            ops.append(P.op("sp", lambda e, c=c: e.dma_start(
                out=hnT_out[c * 128:(c + 1) * 128, t * TW:(t + 1) * TW], in_=self.hn[:, c, :]),
                reads=[("hn", c, b) for b in range(NB)], dma=f"hns{c}"))
        return ops


def build_R(do_p4, do_p1, ntiles=NTOK // TW, max_ops=None):
    nc = bass.Bass("TRN2", target_bir_lowering=False)
    P = Prog(nc)
    din = lambda name, shape, dt=F32: nc.dram_tensor(name, shape, dt, kind="ExternalInput").ap()
    dout = lambda name, shape, dt=F32: nc.dram_tensor(name, shape, dt, kind="ExternalOutput").ap()
    hT = din("hT", [D_MODEL, NTOK])
    gains_d = din("gains", [128, 18, 8])
    hT_out = dout("hT_out", [D_MODEL, NTOK])
    if do_p4:
        oT = din("oT", [D_MODEL, NTOK])
        pT = din("pT", [NTOK // TW, 2, 128, TW])
        w_out = din("w_out", [8, 128, 8, 128])
        wgb = din("wgb", [FC, 128, 8, 128])
        wub = din("wub", [FC, 128, 8, 128])
        wdb = din("wdb", [8, 128, FC, 128])
        w_pg = din("w_pg", [8, 128, 8, 128])
        w_pl = din("w_pl", [8, 128, 2, 128])
    if do_p1:
        wga = din("wga", [FC, 128, 8, 128])
        wua = din("wua", [FC, 128, 8, 128])
        wda = din("wda", [8, 128, FC, 128])
        hn2T = dout("hn2T", [D_MODEL, NTOK], BF16)
    finals = []
    with ExitStack() as es:
        R = RL(nc, P, es, 18)
        R.setup(gains_d)
        for t in range(ntiles):
            R.load_h(hT, t)
            if do_p4:
                R.mixer_out(lambda c, t=t: oT[c * 128:(c + 1) * 128, t * TW:(t + 1) * TW], w_out, 8, 3)
                R.ffn(wgb, wub, wdb, 4, 5)
                R.ple(lambda c, t=t: pT[t, c], w_pg, w_pl, 6, 7)
            if do_p1:
                R.ffn(wga, wua, wda, 9, 10)
            finals += R.store_h(hT_out, t)
            if do_p1:
                R.prenorm(11)
                finals += R.store_hn(hn2T, t)
        nsem = P.emit(finals, max_ops=max_ops)
    return nc


import math

ROPE_THETA = 10000.0
EPS_A = 1e-6
TQ = 512


def mla_consts():
    p = np.arange(128)
    i = (p - 64) % 16
    inv = (ROPE_THETA ** (-(2.0 * i) / 32.0)).astype(np.float32)
    sgn = np.where(((p - 64) % 32) < 16, -1.0, 1.0).astype(np.float32)
    kk = np.arange(128)[:, None, None]
    idx = np.arange(4)[None, :, None]
    qq = np.arange(512)[None, None, :]
    mask = ((128 * (idx - 3) + qq - kk) >= 0).astype(np.float32)
    return np.stack([inv, sgn], 1).astype(np.float32), mask


def dil_consts():
    p = np.arange(128)
    i = p % 32
    inv = (ROPE_THETA ** (-(2.0 * i) / 64.0)).astype(np.float32)
    sgn = np.where((p % 64) < 32, -1.0, 1.0).astype(np.float32)
    kk = np.arange(128)[:, None, None]
    idx = np.arange(20)[None, :, None]
    qq = np.arange(512)[None, None, :]
    d = 128 * (idx - 3) + qq - kk
    m = ((d >= 0) & (d <= 128)).astype(np.float32) + ((d >= 0) & (d % 4 == 0) & (d <= 512)) + \
        ((d >= 0) & (d % 16 == 0) & (d <= 2048))
    return np.stack([inv, sgn], 1).astype(np.float32), m.astype(np.float32)


def tile_k(W):
    K, N = W.shape
    return np.ascontiguousarray(W.reshape(K // 128, 128, N).transpose(1, 0, 2))


def prep_mla_weights(w_in, w_q_up, w_kv_up, q_norm, kv_norm, j):
    heads = range(4 * j, 4 * j + 4)
    wq_lat = tile_k(w_in[:, 0:256])
    wkv_lat = tile_k(w_in[:, 256:384])
    kr = w_in[:, 384:416]
    wkr = np.zeros((2, 1024, 96), np.float32)
    wkr[0, :, 64:96] = kr
    wkr[1, :, 64:80] = kr[:, 16:32]
    wkr[1, :, 80:96] = kr[:, 0:16]
    wkr = np.stack([tile_k(wkr[0]), tile_k(wkr[1])], 1)
    wq_up = np.zeros((4, 2, 256, 96), np.float32)
    wk_up = np.zeros((4, 128, 96), np.float32)
    wv = np.zeros((128, 4, 64), np.float32)
    for n, h in enumerate(heads):
        blk = w_q_up[:, h * 96:(h + 1) * 96]
        wq_up[n, 0] = blk
        wq_up[n, 1, :, 64:80] = blk[:, 80:96]
        wq_up[n, 1, :, 80:96] = blk[:, 64:80]
        wk_up[n, :, 0:64] = w_kv_up[:, h * 128:h * 128 + 64]
        wv[:, n, :] = w_kv_up[:, h * 128 + 64:h * 128 + 128]
    wq_up_t = np.stack([np.stack([tile_k(wq_up[n, s]) for s in range(2)], 1) for n in range(4)], 1)
    wk_up_t = np.ascontiguousarray(wk_up.transpose(1, 0, 2))
    gn = np.zeros((128, 3), np.float32)
    gn[:, 0] = q_norm[0:128]
    gn[:, 1] = q_norm[128:256]
    gn[:, 2] = kv_norm
    return {"wq_lat": wq_lat.reshape(128, -1), "wkv_lat": wkv_lat.reshape(128, -1),
            "wkr": np.ascontiguousarray(wkr).reshape(128, -1),
            "wq_up": np.ascontiguousarray(wq_up_t).reshape(128, -1), "wk_up": wk_up_t.reshape(128, -1),
            "wv": np.ascontiguousarray(wv).reshape(128, -1), "gn": gn}


def prep_dil_weights(w_in, j):
    c2 = 416
    heads = range(4 * j, 4 * j + 4)

    def pairs(base):
        out = np.zeros((2, 2, 1024, 128), np.float32)
        for n, h in enumerate(heads):
            a, o = n // 2, (n % 2) * 64
            blk = w_in[:, base + h * 64: base + (h + 1) * 64]
            out[0, a, :, o:o + 64] = blk
            out[1, a, :, o:o + 32] = blk[:, 32:64]
            out[1, a, :, o + 32:o + 64] = blk[:, 0:32]
        return np.ascontiguousarray(np.stack([np.stack([tile_k(out[s, a]) for a in range(2)], 1) for s in range(2)], 1))
    wqd = pairs(c2)
    wkd = pairs(c2 + 512)
    wvd = np.concatenate([w_in[:, c2 + 1024 + h * 64: c2 + 1024 + (h + 1) * 64] for h in heads], 1)
    return {"wqd": wqd.reshape(128, -1), "wkd": wkd.reshape(128, -1), "wvd": tile_k(wvd).reshape(128, -1)}


def build_A(kind, S=8192, max_ops=None):
    nc = bass.Bass("TRN2", target_bir_lowering=False)
    P = Prog(nc)
    NT = S // TQ
    NBLK = S // 128
    mla = kind == "mla"
    din = lambda name, shape, dt=F32: nc.dram_tensor(name, shape, dt, kind="ExternalInput").ap()
    hn2T = din("hn2T", [1024, S], BF16)
    pos = din("pos", [1, S], I32)
    cst = din("cst", [128, 2])
    nmask = 4 if mla else 20
    maskd = din("mask", [128, nmask * 512])
    oT = nc.dram_tensor("oT", [256, S], F32, kind="ExternalOutput").ap()
    if mla:
        wspec = {"wq_lat": 8 * 256, "wkv_lat": 8 * 128, "wkr": 2 * 8 * 96, "wq_up": 4 * 2 * 2 * 96, "wk_up": 4 * 96,
                 "wv": 256}
        gn_d = din("gn", [128, 3])
    else:
        wspec = {"wqd": 2 * 2 * 8 * 128, "wkd": 2 * 2 * 8 * 128, "wvd": 8 * 256}
    wd = {k: din(k, [128, n]) for k, n in wspec.items()}
    finals = []
    with ExitStack() as es:
        sb = lambda name, shape, dt: es.enter_context(nc.sbuf_tensor("sb_" + name, shape, dt))
        W = {k: sb(k, [128, n], BF16) for k, n in wspec.items()}
        hn2 = sb("hn2", [128, 8, TQ], BF16)
        csts = sb("cst", [128, 2], F32)
        masks = sb("mask", [128, nmask, 512], BF16)
        ones = sb("ones", [128, 128], BF16)
        posi = sb("posi", [128, TQ], I32)
        ang = sb("ang", [128, TQ], F32)
        t_a = sb("t_a", [128, TQ], F32)
        t_k = sb("t_k", [128, TQ], F32)
        t_ki = sb("t_ki", [128, TQ], I32)
        Ct = sb("Ct", [128, TQ], F32)
        St = sb("St", [128, TQ], F32)
        tmp1 = sb("tmp1", [128, 2, TQ], F32)
        tmp2 = sb("tmp2", [128, 2, TQ], F32)
        PT = sb("PT", [128, 6, TQ], BF16)
        osb = sb("osb", [128, 2, TQ], F32)
        rl = sb("rl", [128, TQ], F32)
        rh = sb("rh", [128, TQ], BF16)
        rlo = sb("rlo", [128, TQ], BF16)
        rlb = sb("rlb", [128, TQ], F32)
        V = sb("V", [128, NBLK, 4, 65], BF16)
        if mla:
            K = sb("K", [128, 4, S], BF16)
            Q = sb("Q", [128, 4, TQ], BF16)
            gn = sb("gn", [128, 3], F32)
            zq = sb("zq", [128, 2, TQ], F32)
            zkv = sb("zkv", [128, TQ], F32)
            qn = sb("qn", [128, 2, TQ], BF16)
            kvn = sb("kvn", [128, TQ], BF16)
            sqb = sb("sqb", [128, 2, TQ], BF16)
            rstd = sb("rstd", [128, 2, TQ], F32)
        else:
            K = sb("K", [128, 2, S], BF16)
            Q = sb("Q", [128, 2, TQ], BF16)
        ps = [es.enter_context(nc.psum_tensor(f"ps{i}", [128, 512], F32)) for i in range(8)]
        st = {"bank": 0, "sbank": 0, "pt": 0, "os": 0, "tmp": 0, "sq": 0}

        def nb():
            b = st["bank"]
            st["bank"] = (b + 1) % 3
            return b

        P.op("pool", lambda e: e.memset(ones[:], 1.0), writes=["ones"])
        P.op("pool", lambda e: e.memset(V[:], 1.0), writes=[("V", b) for b in range(NBLK)])
        P.op("sp", lambda e: e.dma_start(out=csts[:], in_=cst), writes=["cst"], dma="cst")
        for k in wspec:
            P.op("pool", lambda e, k=k: e.dma_start(out=W[k][:], in_=wd[k]), writes=[("W", k)], dma="W" + k)
        P.op("pool", lambda e: e.dma_start(out=masks[:].rearrange("p a b -> p (a b)"), in_=maskd), writes=["mask"], dma="mask")
        if mla:
            P.op("sp", lambda e: e.dma_start(out=gn[:], in_=gn_d), writes=["gn"], dma="gn")
        C1 = 6.28125
        C2 = 2 * math.pi - C1

        def tables(i):
            P.op("sp", lambda e: e.dma_start(out=posi[:], in_=pos[0:1, i * TQ:(i + 1) * TQ].partition_broadcast(128)),
                 writes=["posi"], dma="posi")
            P.op("dve", lambda e: e.tensor_copy(out=ang[:], in_=posi[:]), reads=["posi"], writes=["ang"])
            P.op("dve", lambda e: e.tensor_scalar(out=ang[:], in0=ang[:], scalar1=csts[:, 0:1], scalar2=None, op0=ALU.mult),
                 reads=["ang", "cst"], writes=["ang"])
            for which, dst in ((0, St), (1, Ct)):
                D = lambda fn, r, w: P.op("dve", fn, reads=r, writes=w)
                if which == 0:
                    D(lambda e: e.tensor_copy(out=t_a[:], in_=ang[:]), ["ang"], ["t_a"])
                else:
                    D(lambda e: e.tensor_scalar(out=t_a[:], in0=ang[:], scalar1=math.pi / 2, scalar2=None, op0=ALU.add), ["ang"], ["t_a"])
                D(lambda e: e.tensor_scalar(out=t_k[:], in0=t_a[:], scalar1=1.0 / (2 * math.pi), scalar2=None, op0=ALU.mult), ["t_a"], ["t_k"])
                D(lambda e: e.tensor_copy(out=t_ki[:], in_=t_k[:]), ["t_k"], ["t_ki"])
                D(lambda e: e.tensor_copy(out=t_k[:], in_=t_ki[:]), ["t_ki"], ["t_k"])
                D(lambda e: e.scalar_tensor_tensor(out=t_a[:], in0=t_k[:], scalar=-C1, in1=t_a[:], op0=ALU.mult, op1=ALU.add), ["t_k", "t_a"], ["t_a"])
                D(lambda e: e.scalar_tensor_tensor(out=t_a[:], in0=t_k[:], scalar=-C2, in1=t_a[:], op0=ALU.mult, op1=ALU.add), ["t_k", "t_a"], ["t_a"])
                D(lambda e: e.tensor_scalar(out=t_k[:], in0=t_a[:], scalar1=math.pi, scalar2=2 * math.pi, op0=ALU.is_gt, op1=ALU.mult), ["t_a"], ["t_k"])
                D(lambda e: e.tensor_tensor(out=t_a[:], in0=t_a[:], in1=t_k[:], op=ALU.subtract), ["t_a", "t_k"], ["t_a"])
                D(lambda e: e.tensor_scalar(out=t_a[:], in0=t_a[:], scalar1=-math.pi, scalar2=math.pi, op0=ALU.max, op1=ALU.min), ["t_a"], ["t_a"])
                if which == 0:
                    P.op("act", lambda e, dst=dst: e.activation(out=dst[:], in_=t_a[:], func=AF.Sin, scale=csts[:, 1:2]),
                         reads=["t_a", "cst"], writes=["St"])
                else:
                    P.op("act", lambda e, dst=dst: e.activation(out=dst[:], in_=t_a[:], func=AF.Sin),
                         reads=["t_a"], writes=["Ct"])

        def load_hn2(i):
            for c in range(8):
                P.op("sp", lambda e, c=c: e.dma_start(out=hn2[:, c, :], in_=hn2T[c * 128:(c + 1) * 128, i * TQ:(i + 1) * TQ]),
                     writes=[("hn2", c)], dma=f"hn2_{c}")

        def mm(bank, M, parts, reads, ncols=512):
            def f(e):
                ins = None
                for n, (l, r) in enumerate(parts):
                    ins = e.matmul(ps[bank][0:M, 0:ncols], lhsT=l, rhs=r, start=(n == 0), stop=(n == len(parts) - 1))
                return ins
            P.op("pe", f, reads=reads, writes=[("ps", bank)])

        def rope_combine(bA, bB, lo, hi, out_ap, out_res):
            s = st["tmp"]
            st["tmp"] = (s + 1) % 2
            P.op("dve", lambda e: e.tensor_tensor(out=tmp1[lo:hi, s, :], in0=ps[bA][lo:hi, :], in1=Ct[lo:hi, :], op=ALU.mult),
                 reads=[("ps", bA), "Ct"], writes=[("tmp1", s)])
            P.op("dve", lambda e: e.tensor_tensor(out=tmp2[lo:hi, s, :], in0=ps[bB][lo:hi, :], in1=St[lo:hi, :], op=ALU.mult),
                 reads=[("ps", bB), "St"], writes=[("tmp2", s)])
            P.op("pool", lambda e: e.tensor_tensor(out=out_ap, in0=tmp1[lo:hi, s, :], in1=tmp2[lo:hi, s, :], op=ALU.add),
                 reads=[("tmp1", s), ("tmp2", s)], writes=out_res)

        hn2_res = [("hn2", c) for c in range(8)]

        def wv_(name, shape_str, **kw):
            return W[name][:].rearrange(shape_str, **kw)

        def proj_mla(i):
            cs = slice(i * TQ, (i + 1) * TQ)
            wql = wv_("wq_lat", "p (k n) -> p k n", n=256)
            wkvl = wv_("wkv_lat", "p (k n) -> p k n", n=128)
            wkr = wv_("wkr", "p (s k n) -> p s k n", s=2, n=96)
            wqu = wv_("wq_up", "p (h s k n) -> p h s k n", h=4, s=2, n=96)
            wku = wv_("wk_up", "p (h n) -> p h n", n=96)
            wvv = W["wv"]
            for oc in range(2):
                b = nb()
                mm(b, 128, [(wql[:, kc, oc * 128:(oc + 1) * 128], hn2[:, kc, :]) for kc in range(8)], hn2_res + [("W", "wq_lat")])
                P.op("dve", lambda e, b=b, oc=oc: e.tensor_copy(out=zq[:, oc, :], in_=ps[b][:]), reads=[("ps", b)], writes=[("zq", oc)])
                P.op("act", lambda e, oc=oc: e.activation(out=sqb[:, oc, :], in_=zq[:, oc, :], func=AF.Square),
                     reads=[("zq", oc)], writes=[("sqb", oc)])
                P.op("pe", lambda e, oc=oc: e.matmul(ps[7][:], lhsT=ones[:], rhs=sqb[:, oc, :], start=(oc == 0), stop=(oc == 1)),
                     reads=[("sqb", oc), "ones"], writes=[("ps", 7)])
            P.op("act", lambda e: e.activation(out=rstd[:, 0, :], in_=ps[7][:], func=AF.Sqrt, scale=1.0 / 256, bias=EPS_A),
                 reads=[("ps", 7)], writes=[("rstd", 0)])
            P.op("dve", lambda e: e.reciprocal(out=rstd[:, 0, :], in_=rstd[:, 0, :]), reads=[("rstd", 0)], writes=[("rstd", 0)])
            for oc in range(2):
                P.op("dve", lambda e, oc=oc: e.scalar_tensor_tensor(out=qn[:, oc, :], in0=zq[:, oc, :], scalar=gn[:, oc:oc + 1],
                                                                    in1=rstd[:, 0, :], op0=ALU.mult, op1=ALU.mult),
                     reads=[("zq", oc), ("rstd", 0), "gn"], writes=[("qn", oc)])
            b = nb()
            mm(b, 128, [(wkvl[:, kc, :], hn2[:, kc, :]) for kc in range(8)], hn2_res + [("W", "wkv_lat")])
            P.op("dve", lambda e, b=b: e.tensor_copy(out=zkv[:], in_=ps[b][:]), reads=[("ps", b)], writes=["zkv"])
            P.op("act", lambda e: e.activation(out=sqb[:, 0, :], in_=zkv[:], func=AF.Square), reads=["zkv"], writes=[("sqb", 0)])
            P.op("pe", lambda e: e.matmul(ps[7][:], lhsT=ones[:], rhs=sqb[:, 0, :], start=True, stop=True),
                 reads=[("sqb", 0), "ones"], writes=[("ps", 7)])
            P.op("act", lambda e: e.activation(out=rstd[:, 1, :], in_=ps[7][:], func=AF.Sqrt, scale=1.0 / 128, bias=EPS_A),
                 reads=[("ps", 7)], writes=[("rstd", 1)])
            P.op("dve", lambda e: e.reciprocal(out=rstd[:, 1, :], in_=rstd[:, 1, :]), reads=[("rstd", 1)], writes=[("rstd", 1)])
            P.op("dve", lambda e: e.scalar_tensor_tensor(out=kvn[:], in0=zkv[:], scalar=gn[:, 2:3], in1=rstd[:, 1, :],
                                                         op0=ALU.mult, op1=ALU.mult),
                 reads=["zkv", ("rstd", 1), "gn"], writes=["kvn"])
            bA, bB = nb(), nb()
            mm(bA, 96, [(wkr[:, 0, kc, :], hn2[:, kc, :]) for kc in range(8)], hn2_res + [("W", "wkr")])
            mm(bB, 96, [(wkr[:, 1, kc, :], hn2[:, kc, :]) for kc in range(8)], hn2_res + [("W", "wkr")])
            rope_combine(bA, bB, 64, 96, K[64:96, 0, cs], [("Kr", 0, i)])
            for h in range(1, 4):
                P.op("pool", lambda e, h=h: e.tensor_copy(out=K[64:96, h, cs], in_=K[64:96, 0, cs]),
                     reads=[("Kr", 0, i)], writes=[("Kr", h, i)])
            for h in range(4):
                b = nb()
                mm(b, 96, [(wku[:, h, :], kvn[:])], ["kvn", ("W", "wk_up")])
                P.op("act", lambda e, b=b, h=h: e.activation(out=K[0:64, h, cs], in_=ps[b][0:64, :], func=AF.Copy),
                     reads=[("ps", b)], writes=[("Kn", h, i)])
            for blk in range(4):
                b = nb()
                mm(b, 128, [(kvn[:, blk * 128:(blk + 1) * 128], wvv[:, :])], ["kvn", ("W", "wv")], ncols=256)
                bi_ = 4 * i + blk
                P.op("act", lambda e, b=b, bi_=bi_: e.activation(
                    out=V[:, bi_, :, 0:64], in_=ps[b][:, 0:256].rearrange("p (h d) -> p h d", d=64), func=AF.Copy),
                    reads=[("ps", b)], writes=[("V", bi_)])
            for h in range(4):
                bA, bB = nb(), nb()
                mm(bA, 96, [(wqu[:, h, 0, kc, :], qn[:, kc, :]) for kc in range(2)], [("qn", 0), ("qn", 1), ("W", "wq_up")])
                mm(bB, 96, [(wqu[:, h, 1, kc, :], qn[:, kc, :]) for kc in range(2)], [("qn", 0), ("qn", 1), ("W", "wq_up")])
                P.op("dve", lambda e, bA=bA, h=h: e.tensor_copy(out=Q[0:64, h, :], in_=ps[bA][0:64, :]),
                     reads=[("ps", bA)], writes=[("Qn", h)])
                rope_combine(bA, bB, 64, 96, Q[64:96, h, :], [("Qr", h)])

        def proj_dil(i):
            cs = slice(i * TQ, (i + 1) * TQ)
            wq = wv_("wqd", "p (s a k n) -> p s a k n", s=2, a=2, n=128)
            wk = wv_("wkd", "p (s a k n) -> p s a k n", s=2, a=2, n=128)
            wvv = wv_("wvd", "p (k n) -> p k n", n=256)
            for a in range(2):
                bA, bB = nb(), nb()
                mm(bA, 128, [(wq[:, 0, a, kc, :], hn2[:, kc, :]) for kc in range(8)], hn2_res + [("W", "wqd")])
                mm(bB, 128, [(wq[:, 1, a, kc, :], hn2[:, kc, :]) for kc in range(8)], hn2_res + [("W", "wqd")])
                rope_combine(bA, bB, 0, 128, Q[:, a, :], [("Q", a)])
                bA, bB = nb(), nb()
                mm(bA, 128, [(wk[:, 0, a, kc, :], hn2[:, kc, :]) for kc in range(8)], hn2_res + [("W", "wkd")])
                mm(bB, 128, [(wk[:, 1, a, kc, :], hn2[:, kc, :]) for kc in range(8)], hn2_res + [("W", "wkd")])
                rope_combine(bA, bB, 0, 128, K[:, a, cs], [("K", a, i)])
            for blk in range(4):
                b = nb()
                mm(b, 128, [(hn2[:, kc, blk * 128:(blk + 1) * 128], wvv[:, kc, :]) for kc in range(8)],
                   hn2_res + [("W", "wvd")], ncols=256)
                bi_ = 4 * i + blk
                P.op("act", lambda e, b=b, bi_=bi_: e.activation(
                    out=V[:, bi_, :, 0:64], in_=ps[b][:, 0:256].rearrange("p (h d) -> p h d", d=64), func=AF.Copy),
                    reads=[("ps", b)], writes=[("V", bi_)])

        def attend(i, h):
            if mla:
                kbs = list(range(0, 4 * i + 4))
                scale = 96.0 ** -0.5
                qres = [("Qn", h), ("Qr", h)]
            else:
                kbs = list(range(max(0, 4 * i - 16), 4 * i + 4))
                scale = 64.0 ** -0.5
                a, base = h // 2, (h % 2) * 64
                qres = [("Q", a)]
            ob = 5 + (h % 2)
            for n, kb in enumerate(kbs):
                sbk = 3 + st["sbank"]
                st["sbank"] = (st["sbank"] + 1) % 2
                pt = st["pt"]
                st["pt"] = (pt + 1) % 6
                kt = kb // 4
                if mla:
                    lhsT = K[0:96, h, kb * 128:(kb + 1) * 128]
                    rhs = Q[0:96, h, :]
                    kres = [("Kn", h, kt), ("Kr", h, kt)]
                    midx = (4 * i - kb) + 3 if kb >= 4 * i else None
                else:
                    lhsT = K[base:base + 64, a, kb * 128:(kb + 1) * 128]
                    rhs = Q[base:base + 64, a, :]
                    kres = [("K", a, kt)]
                    midx = (4 * i - kb) + 3
                P.op("pe", lambda e, sbk=sbk, lhsT=lhsT, rhs=rhs: e.matmul(ps[sbk][:], lhsT=lhsT, rhs=rhs, start=True, stop=True),
                     reads=kres + qres, writes=[("ps", sbk)])
                P.op("act", lambda e, sbk=sbk, pt=pt: e.activation(out=PT[:, pt, :], in_=ps[sbk][:], func=AF.Exp, scale=scale),
                     reads=[("ps", sbk)], writes=[("PT", pt)])
                if midx is not None:
                    P.op("dve", lambda e, pt=pt, midx=midx: e.tensor_tensor(out=PT[:, pt, :], in0=PT[:, pt, :], in1=masks[:, midx, :],
                                                                          op=ALU.mult),
                         reads=[("PT", pt), "mask"], writes=[("PT", pt)])
                P.op("pe", lambda e, kb=kb, pt=pt, n=n: e.matmul(ps[ob][0:65, :], lhsT=V[:, kb, h, 0:65], rhs=PT[:, pt, :],
                                                                start=(n == 0), stop=(n == len(kbs) - 1)),
                     reads=[("V", kb), ("PT", pt)], writes=[("ps", ob)])

            P.op("dve", lambda e: e.reciprocal(out=rl[64:65, :], in_=ps[ob][64:65, :]), reads=[("ps", ob)], writes=["rl"])
            P.op("dve", lambda e: e.tensor_copy(out=rh[64:65, :], in_=rl[64:65, :]), reads=["rl"], writes=["rh"])
            P.op("dve", lambda e: e.tensor_tensor(out=rlo[64:65, :], in0=rl[64:65, :], in1=rh[64:65, :], op=ALU.subtract),
                 reads=["rl", "rh"], writes=["rlo"])

            def bc(e):
                e.matmul(ps[7][0:64, :], lhsT=ones[64:65, 0:64], rhs=rh[64:65, :], start=True, stop=False)
                return e.matmul(ps[7][0:64, :], lhsT=ones[64:65, 0:64], rhs=rlo[64:65, :], start=False, stop=True)
            P.op("pe", bc, reads=["rh", "rlo", "ones"], writes=[("ps", 7)])
            P.op("act", lambda e: e.activation(out=rlb[0:64, :], in_=ps[7][0:64, :], func=AF.Copy), reads=[("ps", 7)], writes=["rlb"])
            os_ = st["os"]
            st["os"] = (os_ + 1) % 2
            P.op("dve", lambda e: e.tensor_tensor(out=osb[0:64, os_, :], in0=ps[ob][0:64, :], in1=rlb[0:64, :], op=ALU.mult),
                 reads=[("ps", ob), "rlb"], writes=[("osb", os_)])
            finals.append(P.op("sp", lambda e: e.dma_start(out=oT[h * 64:(h + 1) * 64, i * TQ:(i + 1) * TQ], in_=osb[0:64, os_, :]),
                               reads=[("osb", os_)], dma=f"o{os_}"))

        for i in range(NT):
            load_hn2(i)
            tables(i)
            if mla:
                proj_mla(i)
            else:
                proj_dil(i)
            for h in range(4):
                attend(i, h)
        P.emit(finals, max_ops=max_ops)
    return nc


from concourse.bass_utils import run_bass_kernel_spmd

_PROGS = {}


def _prog(key, fn):
    if key not in _PROGS:
        _PROGS[key] = fn()
    return _PROGS[key]


def _run(nc, in_maps):
    res = run_bass_kernel_spmd(nc, in_maps, core_ids=list(range(8)))
    return res.results


def kernel(x, p, positions, norm_gains, w_in, q_norm, w_q_up, kv_norm, w_kv_up,
           group_out_norm, w_out, ffn_gate, ffn_up, ffn_down, w_ple, w_ple_gate):
    f32 = lambda a: np.ascontiguousarray(np.asarray(a, dtype=np.float32))
    x, p, norm_gains, w_in, q_norm, w_q_up, kv_norm, w_kv_up = map(f32, (x, p, norm_gains, w_in, q_norm, w_q_up, kv_norm, w_kv_up))
    group_out_norm, w_out, ffn_gate, ffn_up, ffn_down, w_ple, w_ple_gate = map(
        f32, (group_out_norm, w_out, ffn_gate, ffn_up, ffn_down, w_ple, w_ple_gate))
    positions = np.ascontiguousarray(np.asarray(positions, dtype=np.int32))
    B, S, D = x.shape
    L = norm_gains.shape[0]
    cores = [(b, j) for b in range(B) for j in range(2)]

    def gains_pack(lp, ln):
        g = np.zeros((18, D), np.float32)
        if lp is not None:
            g[0:8] = norm_gains[lp]
            g[8] = group_out_norm[lp]
        if ln is not None:
            g[9:17] = norm_gains[ln]
            g[17] = group_out_norm[ln]
        return prep_vec(g)

    def ffn_w(l, s, sfx):
        return {"wg" + sfx: prep_w(ffn_gate[l, s]), "wu" + sfx: prep_w(ffn_up[l, s]), "wd" + sfx: prep_w(ffn_down[l, s])}

    def p4_w(l):
        d = {"w_out": prep_w(w_out[l]), "w_pg": prep_w(w_ple_gate[l]), "w_pl": prep_w(w_ple[l])}
        d.update(ffn_w(l, 1, "b"))
        return d

    mc, mm_ = mla_consts()
    dc, dm_ = dil_consts()
    mm_ = np.ascontiguousarray(mm_.reshape(128, -1))
    dm_ = np.ascontiguousarray(dm_.reshape(128, -1))

    def attention(l, hn2_shards):
        full = [np.ascontiguousarray(np.concatenate([hn2_shards[2 * b], hn2_shards[2 * b + 1]], axis=1)) for b in range(B)]
        mw = [prep_mla_weights(w_in[l], w_q_up[l], w_kv_up[l], q_norm[l], kv_norm[l], j) for j in range(2)]
        dw = [prep_dil_weights(w_in[l], j) for j in range(2)]
        ims_m, ims_d = [], []
        for (b, j) in cores:
            base = {"hn2T": full[b], "pos": positions[b][None, :]}
            ims_m.append(dict(base, cst=mc, mask=mm_, **mw[j]))
            ims_d.append(dict(base, cst=dc, mask=dm_, **dw[j]))
        om = _run(_prog("Am", lambda: build_A("mla", S)), ims_m)
        od = _run(_prog("Ad", lambda: build_A("dil", S)), ims_d)
        outs = []
        for b in range(B):
            o_full = np.concatenate([om[2 * b]["oT"], om[2 * b + 1]["oT"], od[2 * b]["oT"], od[2 * b + 1]["oT"]], axis=0)
            for j in range(2):
                outs.append(np.ascontiguousarray(o_full[:, j * NTOK:(j + 1) * NTOK]))
        return outs

    hT = [np.ascontiguousarray(x[b, j * NTOK:(j + 1) * NTOK].T) for (b, j) in cores]
    g = gains_pack(None, 0)
    wa = ffn_w(0, 0, "a")
    r = _run(_prog("R0", lambda: build_R(False, True)), [dict(hT=hT[c], gains=g, **wa) for c in range(8)])
    hT = [r[c]["hT_out"] for c in range(8)]
    hn2 = [r[c]["hn2T"] for c in range(8)]
    for l in range(L):
        oT = attention(l, hn2)
        last = l == L - 1
        wp = p4_w(l)
        pT = [np.ascontiguousarray(p[l, b, j * NTOK:(j + 1) * NTOK].T.reshape(2, 128, NTOK // TW, TW).transpose(2, 0, 1, 3)) for (b, j) in cores]
        if last:
            g = gains_pack(l, None)
            r = _run(_prog("R2", lambda: build_R(True, False)),
                     [dict(hT=hT[c], gains=g, oT=oT[c], pT=pT[c], **wp) for c in range(8)])
        else:
            g = gains_pack(l, l + 1)
            wa = ffn_w(l + 1, 0, "a")
            r = _run(_prog("R1", lambda: build_R(True, True)),
                     [dict(hT=hT[c], gains=g, oT=oT[c], pT=pT[c], **wp, **wa) for c in range(8)])
            hn2 = [r[c]["hn2T"] for c in range(8)]
        hT = [r[c]["hT_out"] for c in range(8)]
    out = np.empty((B, S, D), np.float32)
    for c, (b, j) in enumerate(cores):
        out[b, j * NTOK:(j + 1) * NTOK, :] = np.asarray(hT[c], np.float32).T
    return out
```

```python
import numpy as np
import concourse.bass as bass
import concourse.mybir as mybir

F32 = mybir.dt.float32
BF16 = mybir.dt.bfloat16
I32 = mybir.dt.int32
AF = mybir.ActivationFunctionType
ALU = mybir.AluOpType

SEM_LIMIT = 30000


class _Op:
    __slots__ = ("eng", "fn", "deps", "is_dma", "semkey", "sem", "count", "signal", "idx")


class Prog:
    ENG = ("pe", "act", "dve", "pool", "sp")

    def __init__(self, nc):
        self.nc = nc
        self.ops = []
        self.res = {}

    def op(self, eng, fn, reads=(), writes=(), dma=None):
        o = _Op()
        o.eng = eng
        o.fn = fn
        o.is_dma = dma is not None
        o.semkey = ("dma", dma) if o.is_dma else ("eng", eng)
        o.signal = False
        o.idx = len(self.ops)
        deps = {}
        res = self.res
        for r in reads:
            e = res.get(r)
            if e is not None and e[0] is not None:
                deps[e[0].idx] = e[0]
        for w in writes:
            e = res.get(w)
            if e is not None:
                if e[0] is not None:
                    deps[e[0].idx] = e[0]
                for rd in e[1]:
                    deps[rd.idx] = rd
        for r in reads:
            e = res.get(r)
            if e is None:
                res[r] = [None, [o]]
            else:
                e[1].append(o)
        for w in writes:
            res[w] = [o, []]
        o.deps = [d for d in deps.values() if d is not o]
        for d in o.deps:
            d.signal = True
        self.ops.append(o)
        return o

    def emit(self, final_wait_ops=(), max_ops=None):
        nc = self.nc
        if max_ops is not None:
            self.ops = self.ops[:max_ops]
            final_wait_ops = [o for o in self.ops if o.is_dma]
        import contextlib
        stack = contextlib.ExitStack()
        sems = {}
        counts = {}
        nsem = [0]

        def new_sem():
            nsem[0] += 1
            return stack.enter_context(nc.semaphore(f"s{nsem[0]}"))

        for o in self.ops:
            if o.is_dma:
                o.signal = True
            if not o.signal:
                continue
            k = o.semkey
            step = 16 if o.is_dma else 1
            if k not in sems or counts[k] + step > SEM_LIMIT:
                sems[k] = new_sem()
                counts[k] = 0
            counts[k] += step
            o.sem = sems[k]
            o.count = counts[k]
        per_eng = {e: [] for e in self.ENG}
        for o in self.ops:
            per_eng[o.eng].append(o)
        engobj = {"pe": "tensor", "act": "scalar", "dve": "vector", "pool": "gpsimd", "sp": "sync"}
        finals = list(final_wait_ops)
        with stack:
            with nc.Block() as block:
                for ename in self.ENG:
                    ops = per_eng[ename]

                    def body(e, ops=ops, ename=ename):
                        waited = {}
                        for o in ops:
                            need = {}
                            for d in o.deps:
                                key = id(d.sem)
                                if waited.get(key, 0) >= d.count:
                                    continue
                                if key not in need or need[key][1] < d.count:
                                    need[key] = (d.sem, d.count)
                            for key, (s, c) in need.items():
                                e.wait_ge(s, c)
                                waited[key] = c
                            ins = o.fn(e)
                            if o.signal:
                                ins.then_inc(o.sem, 16 if o.is_dma else 1)
                        if ename == "sp":
                            need = {}
                            for d in finals:
                                key = id(d.sem)
                                if key not in need or need[key][1] < d.count:
                                    need[key] = (d.sem, d.count)
                            for key, (s, c) in need.items():
                                e.wait_ge(s, c)

                    getattr(block, engobj[ename])(body)
        return nsem[0]


from contextlib import ExitStack

D_MODEL = 1024
D_FF = 2816
FC = 22
EPS = 1e-6
TW = 1024
NB = TW // 512
NTOK = 4096


def prep_w(W):
    K, N = W.shape
    return np.ascontiguousarray(W.reshape(K // 128, 128, N // 128, 128).transpose(2, 1, 0, 3))


def prep_vec(v):
    sh = v.shape
    C = sh[-1] // 128
    v2 = v.reshape(-1, C, 128)
    return np.ascontiguousarray(v2.transpose(2, 0, 1))


class RL:
    def __init__(self, nc, P, es, nvec):
        self.nc, self.P = nc, P
        sb = lambda name, shape, dt: es.enter_context(nc.sbuf_tensor("sb_" + name, shape, dt))
        self.h = sb("h", [128, 8, TW], F32)
        self.hn = sb("hn", [128, 8, TW], BF16)
        self.act = sb("act", [128, FC, TW], BF16)
        self.fbuf = sb("fbuf", [128, 8, TW], F32)
        self.sqb = sb("sqb", [128, 4, 512], BF16)
        self.rstd = sb("rstd", [128, 2, TW], F32)
        self.sg = sb("sg", [128, 2, 512], BF16)
        self.tmp = sb("tmp", [128, 2, 512], F32)
        self.wbuf = sb("wbuf", [128, 6, FC * 128], BF16)
        self.gains = sb("gains", [128, nvec, 8], F32)
        self.ones = sb("ones", [128, 128], BF16)
        self.pbf = sb("pbf", [128, 2, TW], BF16)
        self.ps = [es.enter_context(nc.psum_tensor(f"ps{i}", [128, 512], F32)) for i in range(8)]
        self.wslot = 0
        self.bank = 0
        self.sqslot = 0
        self.sgslot = 0
        self.tmpslot = 0

    def setup(self, gains_dram):
        P = self.P
        P.op("pool", lambda e: e.memset(self.ones[:], 1.0), writes=["ones"])
        P.op("sp", lambda e: e.dma_start(out=self.gains[:], in_=gains_dram), writes=["gains"], dma="gains")

    def next_bank(self):
        b = self.bank
        self.bank = (self.bank + 1) % 6
        return b

    def stats(self, src_fn, src_res, blk, first, last, bank):
        P = self.P
        s = self.sqslot
        self.sqslot = (s + 1) % 4
        sq = self.sqb[:, s, :]
        P.op("act", lambda e: e.activation(out=sq, in_=src_fn(), func=AF.Square),
             reads=src_res, writes=[("sqb", s)])
        P.op("pe", lambda e: e.matmul(self.ps[bank][:], lhsT=self.ones[:], rhs=sq, start=first, stop=last),
             reads=[("sqb", s), "ones"], writes=[("ps", bank)])

    def rstd_from(self, bank, slot, blk, n, scale=1.0):
        P = self.P
        r = self.rstd[:, slot, blk * 512:(blk + 1) * 512]
        s2 = 1.0 / (scale * scale)
        P.op("act", lambda e: e.activation(out=r, in_=self.ps[bank][:], func=AF.Sqrt, scale=s2 / n, bias=EPS * s2),
             reads=[("ps", bank)], writes=[("rstd", slot, blk)])
        P.op("dve", lambda e: e.reciprocal(out=r, in_=r), reads=[("rstd", slot, blk)], writes=[("rstd", slot, blk)])

    def prenorm(self, gi, src="h"):
        P = self.P
        srcT = self.h if src == "h" else self.fbuf
        for blk in range(NB):
            cs = slice(blk * 512, (blk + 1) * 512)
            bank = 6 + blk % 2
            for c in range(8):
                self.stats(lambda c=c, cs=cs: srcT[:, c, cs], [(src, c, blk)], blk, c == 0, c == 7, bank)
            self.rstd_from(bank, 0, blk, D_MODEL)
            for c in range(8):
                P.op("dve", lambda e, c=c, cs=cs: e.scalar_tensor_tensor(
                    out=self.hn[:, c, cs], in0=srcT[:, c, cs], scalar=self.gains[:, gi, c:c + 1],
                    in1=self.rstd[:, 0, cs], op0=ALU.mult, op1=ALU.mult),
                    reads=[(src, c, blk), ("rstd", 0, blk), "gains"], writes=[("hn", c, blk)])

    def postnorm_add(self, gi, scale):
        P = self.P
        for blk in range(NB):
            cs = slice(blk * 512, (blk + 1) * 512)
            bank = 6 + blk % 2
            self.rstd_from(bank, 1, blk, D_MODEL, scale)
            for c in range(8):
                ts = self.tmpslot
                self.tmpslot = (ts + 1) % 2
                P.op("dve", lambda e, c=c, ts=ts, cs=cs: e.scalar_tensor_tensor(
                    out=self.tmp[:, ts, :], in0=self.fbuf[:, c, cs], scalar=self.gains[:, gi, c:c + 1],
                    in1=self.rstd[:, 1, cs], op0=ALU.mult, op1=ALU.mult),
                    reads=[("fbuf", c, blk), ("rstd", 1, blk), "gains"], writes=[("tmp", ts)])
                P.op("pool", lambda e, c=c, ts=ts, cs=cs: e.tensor_tensor(
                    out=self.h[:, c, cs], in0=self.h[:, c, cs], in1=self.tmp[:, ts, :], op=ALU.add),
                    reads=[("h", c, blk), ("tmp", ts)], writes=[("h", c, blk)])

    def linear(self, w_tile_fn, KC, n_oc, rhs_fn, rhs_res_fn, evac, M=128):
        P = self.P
        for oc in range(n_oc):
            ws = self.wslot
            self.wslot = (ws + 1) % 6
            wt = self.wbuf[:, ws, 0:KC * M].rearrange("p (k m) -> p k m", m=M)
            wflat = self.wbuf[:, ws, 0:KC * M]
            P.op("pool", lambda e, oc=oc, wflat=wflat: e.dma_start(out=wflat, in_=w_tile_fn(oc).rearrange("p k m -> p (k m)")),
                 writes=[("w", ws)], dma=f"w{ws}")
            for blk in range(NB):
                bank = self.next_bank()

                def mm(e, wt=wt, blk=blk, bank=bank):
                    ins = None
                    for kc in range(KC):
                        ins = e.matmul(self.ps[bank][0:M, :], lhsT=wt[:, kc, :], rhs=rhs_fn(kc, blk),
                                       start=(kc == 0), stop=(kc == KC - 1))
                    return ins
                rr = [("w", ws)]
                for kc in range(KC):
                    rr += rhs_res_fn(kc, blk)
                P.op("pe", mm, reads=rr, writes=[("ps", bank)])
                evac(oc, blk, bank)

    def evac_to_fbuf(self, oc, blk, bank):
        P = self.P
        cs = slice(blk * 512, (blk + 1) * 512)
        P.op("dve", lambda e: e.tensor_copy(out=self.fbuf[:, oc, cs], in_=self.ps[bank][:]),
             reads=[("ps", bank)], writes=[("fbuf", oc, blk)])
        self.stats(lambda: self.fbuf[:, oc, cs], [("fbuf", oc, blk)], blk, oc == 0, oc == 7, 6 + blk % 2)

    def ffn(self, wg, wu, wd, g_pre, g_post):
        P = self.P
        self.prenorm(g_pre)
        hn_fn = lambda kc, blk: self.hn[:, kc, blk * 512:(blk + 1) * 512]
        hn_res = lambda kc, blk: [("hn", kc, blk)]
        for fc in range(FC):
            def evac_gate(oc, blk, bank):
                s = self.sgslot
                self.sgslot = (s + 1) % 2
                self._sg_for[(oc, blk)] = s
                P.op("act", lambda e: e.activation(out=self.sg[:, s, :], in_=self.ps[bank][:], func=AF.Silu),
                     reads=[("ps", bank)], writes=[("sg", s)])

            def evac_up(oc, blk, bank, fc=fc):
                s = self._sg_for[(oc, blk)]
                cs = slice(blk * 512, (blk + 1) * 512)
                P.op("dve", lambda e: e.tensor_tensor(out=self.act[:, fc, cs], in0=self.ps[bank][:],
                                                      in1=self.sg[:, s, :], op=ALU.mult),
                     reads=[("ps", bank), ("sg", s)], writes=[("act", fc, blk)])
            self._sg_for = {}
            self.linear(lambda oc, fc=fc: wg[fc], 8, 1, hn_fn, hn_res, evac_gate)
            self.linear(lambda oc, fc=fc: wu[fc], 8, 1, hn_fn, hn_res, evac_up)
        act_fn = lambda kc, blk: self.act[:, kc, blk * 512:(blk + 1) * 512]
        act_res = lambda kc, blk: [("act", kc, blk)]
        self.linear(lambda oc: wd[oc], FC, 8, act_fn, act_res, self.evac_to_fbuf)
        self.postnorm_add(g_post, 0.5)

    def mixer_out(self, oT_tile, w_out, gg, g_post):
        P = self.P
        for c in range(8):
            P.op("sp", lambda e, c=c: e.dma_start(out=self.fbuf[:, c, :], in_=oT_tile(c)),
                 writes=[("fbuf", c, b) for b in range(NB)], dma=f"o{c}")
        for blk in range(NB):
            cs = slice(blk * 512, (blk + 1) * 512)
            for grp in range(2):
                bank = 6 + grp
                for c4 in range(4):
                    c = grp * 4 + c4
                    self.stats(lambda c=c, cs=cs: self.fbuf[:, c, cs], [("fbuf", c, blk)], blk, c4 == 0, c4 == 3, bank)
                self.rstd_from(bank, grp, blk, 512)
                for c4 in range(4):
                    c = grp * 4 + c4
                    P.op("dve", lambda e, c=c, grp=grp, cs=cs: e.scalar_tensor_tensor(
                        out=self.hn[:, c, cs], in0=self.fbuf[:, c, cs], scalar=self.gains[:, gg, c:c + 1],
                        in1=self.rstd[:, grp, cs], op0=ALU.mult, op1=ALU.mult),
                        reads=[("fbuf", c, blk), ("rstd", grp, blk), "gains"], writes=[("hn", c, blk)])
        hn_fn = lambda kc, blk: self.hn[:, kc, blk * 512:(blk + 1) * 512]
        hn_res = lambda kc, blk: [("hn", kc, blk)]
        self.linear(lambda oc: w_out[oc], 8, 8, hn_fn, hn_res, self.evac_to_fbuf)
        self.postnorm_add(g_post, 1.0)

    def ple(self, pT_tile, w_gate, w_ple, g_pre, g_post):
        P = self.P
        for c in range(2):
            P.op("pool", lambda e, c=c: e.dma_start(out=self.pbf[:, c, :], in_=pT_tile(c)), writes=["pbf"], dma=f"pbf{c}")
        self.prenorm(g_pre)
        hn_fn = lambda kc, blk: self.hn[:, kc, blk * 512:(blk + 1) * 512]
        hn_res = lambda kc, blk: [("hn", kc, blk)]

        def evac_gate(oc, blk, bank):
            cs = slice(blk * 512, (blk + 1) * 512)
            P.op("act", lambda e: e.activation(out=self.fbuf[:, oc, cs], in_=self.ps[bank][:], func=AF.Sigmoid),
                 reads=[("ps", bank)], writes=[("fbuf", oc, blk)])
        self.linear(lambda oc: w_gate[oc], 8, 8, hn_fn, hn_res, evac_gate)

        def evac_p(oc, blk, bank):
            cs = slice(blk * 512, (blk + 1) * 512)
            P.op("dve", lambda e: e.tensor_tensor(out=self.fbuf[:, oc, cs], in0=self.ps[bank][:],
                                                  in1=self.fbuf[:, oc, cs], op=ALU.mult),
                 reads=[("ps", bank), ("fbuf", oc, blk)], writes=[("fbuf", oc, blk)])
            self.stats(lambda: self.fbuf[:, oc, cs], [("fbuf", oc, blk)], blk, oc == 0, oc == 7, 6 + blk % 2)
        p_fn = lambda kc, blk: self.pbf[:, kc, blk * 512:(blk + 1) * 512]
        self.linear(lambda oc: w_ple[oc], 2, 8, p_fn, lambda kc, blk: ["pbf"], evac_p)
        self.postnorm_add(g_post, 1.0)

    def load_h(self, hT, t):
        P = self.P
        for c in range(8):
            P.op("sp", lambda e, c=c: e.dma_start(
                out=self.h[:, c, :], in_=hT[c * 128:(c + 1) * 128, t * TW:(t + 1) * TW]),
                writes=[("h", c, b) for b in range(NB)], dma=f"h{c}")

    def store_h(self, hT_out, t):
        P = self.P
        ops = []
        for c in range(8):
            ops.append(P.op("sp", lambda e, c=c: e.dma_start(
                out=hT_out[c * 128:(c + 1) * 128, t * TW:(t + 1) * TW], in_=self.h[:, c, :]),
                reads=[("h", c, b) for b in range(NB)], dma=f"hs{c}"))
        return ops

    def store_hn(self, hnT_out, t):
        P = self.P
        ops = []
        for c in range(8):
            ops.append(P.op("sp", lambda e, c=c: e.dma_start(
                out=hnT_out[c * 128:(c + 1) * 128, t * TW:(t + 1) * TW], in_=self.hn[:, c, :]),
                reads=[("hn", c, b) for b in range(NB)], dma=f"hns{c}"))
        return ops


def build_R(do_p4, do_p1, ntiles=NTOK // TW, max_ops=None):
    nc = bass.Bass("TRN2", target_bir_lowering=False)
    P = Prog(nc)
    din = lambda name, shape, dt=F32: nc.dram_tensor(name, shape, dt, kind="ExternalInput").ap()
    dout = lambda name, shape, dt=F32: nc.dram_tensor(name, shape, dt, kind="ExternalOutput").ap()
    hT = din("hT", [D_MODEL, NTOK])
    gains_d = din("gains", [128, 18, 8])
    hT_out = dout("hT_out", [D_MODEL, NTOK])
    if do_p4:
        oT = din("oT", [D_MODEL, NTOK])
        pT = din("pT", [NTOK // TW, 2, 128, TW])
        w_out = din("w_out", [8, 128, 8, 128])
        wgb = din("wgb", [FC, 128, 8, 128])
        wub = din("wub", [FC, 128, 8, 128])
        wdb = din("wdb", [8, 128, FC, 128])
        w_pg = din("w_pg", [8, 128, 8, 128])
        w_pl = din("w_pl", [8, 128, 2, 128])
    if do_p1:
        wga = din("wga", [FC, 128, 8, 128])
        wua = din("wua", [FC, 128, 8, 128])
        wda = din("wda", [8, 128, FC, 128])
        hn2T = dout("hn2T", [D_MODEL, NTOK], BF16)
    finals = []
    with ExitStack() as es:
        R = RL(nc, P, es, 18)
        R.setup(gains_d)
        for t in range(ntiles):
            R.load_h(hT, t)
            if do_p4:
                R.mixer_out(lambda c, t=t: oT[c * 128:(c + 1) * 128, t * TW:(t + 1) * TW], w_out, 8, 3)
                R.ffn(wgb, wub, wdb, 4, 5)
                R.ple(lambda c, t=t: pT[t, c], w_pg, w_pl, 6, 7)
            if do_p1:
                R.ffn(wga, wua, wda, 9, 10)
            finals += R.store_h(hT_out, t)
            if do_p1:
                R.prenorm(11)
                finals += R.store_hn(hn2T, t)
        nsem = P.emit(finals, max_ops=max_ops)
    return nc


import math

ROPE_THETA = 10000.0
EPS_A = 1e-6
TQ = 512


def mla_consts():
    p = np.arange(128)
    i = (p - 64) % 16
    inv = (ROPE_THETA ** (-(2.0 * i) / 32.0)).astype(np.float32)
    sgn = np.where(((p - 64) % 32) < 16, -1.0, 1.0).astype(np.float32)
    kk = np.arange(128)[:, None, None]
    idx = np.arange(4)[None, :, None]
    qq = np.arange(512)[None, None, :]
    mask = ((128 * (idx - 3) + qq - kk) >= 0).astype(np.float32)
    return np.stack([inv, sgn], 1).astype(np.float32), mask


def dil_consts():
    p = np.arange(128)
    i = p % 32
    inv = (ROPE_THETA ** (-(2.0 * i) / 64.0)).astype(np.float32)
    sgn = np.where((p % 64) < 32, -1.0, 1.0).astype(np.float32)
    kk = np.arange(128)[:, None, None]
    idx = np.arange(20)[None, :, None]
    qq = np.arange(512)[None, None, :]
    d = 128 * (idx - 3) + qq - kk
    m = ((d >= 0) & (d <= 128)).astype(np.float32) + ((d >= 0) & (d % 4 == 0) & (d <= 512)) + \
        ((d >= 0) & (d % 16 == 0) & (d <= 2048))
    return np.stack([inv, sgn], 1).astype(np.float32), m.astype(np.float32)


def tile_k(W):
    K, N = W.shape
    return np.ascontiguousarray(W.reshape(K // 128, 128, N).transpose(1, 0, 2))


def prep_mla_weights(w_in, w_q_up, w_kv_up, q_norm, kv_norm, j):
    heads = range(4 * j, 4 * j + 4)
    wq_lat = tile_k(w_in[:, 0:256])
    wkv_lat = tile_k(w_in[:, 256:384])
    kr = w_in[:, 384:416]
    wkr = np.zeros((2, 1024, 96), np.float32)
    wkr[0, :, 64:96] = kr
    wkr[1, :, 64:80] = kr[:, 16:32]
    wkr[1, :, 80:96] = kr[:, 0:16]
    wkr = np.stack([tile_k(wkr[0]), tile_k(wkr[1])], 1)
    wq_up = np.zeros((4, 2, 256, 96), np.float32)
    wk_up = np.zeros((4, 128, 96), np.float32)
    wv = np.zeros((128, 4, 64), np.float32)
    for n, h in enumerate(heads):
        blk = w_q_up[:, h * 96:(h + 1) * 96]
        wq_up[n, 0] = blk
        wq_up[n, 1, :, 64:80] = blk[:, 80:96]
        wq_up[n, 1, :, 80:96] = blk[:, 64:80]
        wk_up[n, :, 0:64] = w_kv_up[:, h * 128:h * 128 + 64]
        wv[:, n, :] = w_kv_up[:, h * 128 + 64:h * 128 + 128]
    wq_up_t = np.stack([np.stack([tile_k(wq_up[n, s]) for s in range(2)], 1) for n in range(4)], 1)
    wk_up_t = np.ascontiguousarray(wk_up.transpose(1, 0, 2))
    gn = np.zeros((128, 3), np.float32)
    gn[:, 0] = q_norm[0:128]
    gn[:, 1] = q_norm[128:256]
    gn[:, 2] = kv_norm
    return {"wq_lat": wq_lat.reshape(128, -1), "wkv_lat": wkv_lat.reshape(128, -1),
            "wkr": np.ascontiguousarray(wkr).reshape(128, -1),
            "wq_up": np.ascontiguousarray(wq_up_t).reshape(128, -1), "wk_up": wk_up_t.reshape(128, -1),
            "wv": np.ascontiguousarray(wv).reshape(128, -1), "gn": gn}


def prep_dil_weights(w_in, j):
    c2 = 416
    heads = range(4 * j, 4 * j + 4)

    def pairs(base):
        out = np.zeros((2, 2, 1024, 128), np.float32)
        for n, h in enumerate(heads):
            a, o = n // 2, (n % 2) * 64
            blk = w_in[:, base + h * 64: base + (h + 1) * 64]
            out[0, a, :, o:o + 64] = blk
            out[1, a, :, o:o + 32] = blk[:, 32:64]
            out[1, a, :, o + 32:o + 64] = blk[:, 0:32]
        return np.ascontiguousarray(np.stack([np.stack([tile_k(out[s, a]) for a in range(2)], 1) for s in range(2)], 1))
    wqd = pairs(c2)
    wkd = pairs(c2 + 512)
    wvd = np.concatenate([w_in[:, c2 + 1024 + h * 64: c2 + 1024 + (h + 1) * 64] for h in heads], 1)
    return {"wqd": wqd.reshape(128, -1), "wkd": wkd.reshape(128, -1), "wvd": tile_k(wvd).reshape(128, -1)}


def build_A(kind, S=8192, max_ops=None):
    nc = bass.Bass("TRN2", target_bir_lowering=False)
    P = Prog(nc)
    NT = S // TQ
    NBLK = S // 128
    mla = kind == "mla"
    din = lambda name, shape, dt=F32: nc.dram_tensor(name, shape, dt, kind="ExternalInput").ap()
    hn2T = din("hn2T", [1024, S], BF16)
    pos = din("pos", [1, S], I32)
    cst = din("cst", [128, 2])
    nmask = 4 if mla else 20
    maskd = din("mask", [128, nmask * 512])
    oT = nc.dram_tensor("oT", [256, S], F32, kind="ExternalOutput").ap()
    if mla:
        wspec = {"wq_lat": 8 * 256, "wkv_lat": 8 * 128, "wkr": 2 * 8 * 96, "wq_up": 4 * 2 * 2 * 96, "wk_up": 4 * 96,
                 "wv": 256}
        gn_d = din("gn", [128, 3])
    else:
        wspec = {"wqd": 2 * 2 * 8 * 128, "wkd": 2 * 2 * 8 * 128, "wvd": 8 * 256}
    wd = {k: din(k, [128, n]) for k, n in wspec.items()}
    finals = []
    with ExitStack() as es:
        sb = lambda name, shape, dt: es.enter_context(nc.sbuf_tensor("sb_" + name, shape, dt))
        W = {k: sb(k, [128, n], BF16) for k, n in wspec.items()}
        hn2 = sb("hn2", [128, 8, TQ], BF16)
        csts = sb("cst", [128, 2], F32)
        masks = sb("mask", [128, nmask, 512], BF16)
        ones = sb("ones", [128, 128], BF16)
        posi = sb("posi", [128, TQ], I32)
        ang = sb("ang", [128, TQ], F32)
        t_a = sb("t_a", [128, TQ], F32)
        t_k = sb("t_k", [128, TQ], F32)
        t_ki = sb("t_ki", [128, TQ], I32)
        Ct = sb("Ct", [128, TQ], F32)
        St = sb("St", [128, TQ], F32)
        tmp1 = sb("tmp1", [128, 2, TQ], F32)
        tmp2 = sb("tmp2", [128, 2, TQ], F32)
        PT = sb("PT", [128, 6, TQ], BF16)
        osb = sb("osb", [128, 2, TQ], F32)
        rl = sb("rl", [128, TQ], F32)
        rh = sb("rh", [128, TQ], BF16)
        rlo = sb("rlo", [128, TQ], BF16)
        rlb = sb("rlb", [128, TQ], F32)
        V = sb("V", [128, NBLK, 4, 65], BF16)
        if mla:
            K = sb("K", [128, 4, S], BF16)
            Q = sb("Q", [128, 4, TQ], BF16)
            gn = sb("gn", [128, 3], F32)
            zq = sb("zq", [128, 2, TQ], F32)
            zkv = sb("zkv", [128, TQ], F32)
            qn = sb("qn", [128, 2, TQ], BF16)
            kvn = sb("kvn", [128, TQ], BF16)
            sqb = sb("sqb", [128, 2, TQ], BF16)
            rstd = sb("rstd", [128, 2, TQ], F32)
        else:
            K = sb("K", [128, 2, S], BF16)
            Q = sb("Q", [128, 2, TQ], BF16)
        ps = [es.enter_context(nc.psum_tensor(f"ps{i}", [128, 512], F32)) for i in range(8)]
        st = {"bank": 0, "sbank": 0, "pt": 0, "os": 0, "tmp": 0, "sq": 0}

        def nb():
            b = st["bank"]
            st["bank"] = (b + 1) % 2
            return b

        P.op("pool", lambda e: e.memset(ones[:], 1.0), writes=["ones"])
        P.op("pool", lambda e: e.memset(V[:], 1.0), writes=[("V", b) for b in range(NBLK)])
        P.op("sp", lambda e: e.dma_start(out=csts[:], in_=cst), writes=["cst"], dma="cst")
        for k in wspec:
            P.op("pool", lambda e, k=k: e.dma_start(out=W[k][:], in_=wd[k]), writes=[("W", k)], dma="W" + k)
        P.op("pool", lambda e: e.dma_start(out=masks[:].rearrange("p a b -> p (a b)"), in_=maskd), writes=["mask"], dma="mask")
        if mla:
            P.op("sp", lambda e: e.dma_start(out=gn[:], in_=gn_d), writes=["gn"], dma="gn")
        C1 = 6.28125
        C2 = 2 * math.pi - C1

        def tables(i):
            P.op("sp", lambda e: e.dma_start(out=posi[:], in_=pos[0:1, i * TQ:(i + 1) * TQ].partition_broadcast(128)),
                 writes=["posi"], dma="posi")
            P.op("dve", lambda e: e.tensor_copy(out=ang[:], in_=posi[:]), reads=["posi"], writes=["ang"])
            P.op("dve", lambda e: e.tensor_scalar(out=ang[:], in0=ang[:], scalar1=csts[:, 0:1], scalar2=None, op0=ALU.mult),
                 reads=["ang", "cst"], writes=["ang"])
            for which, dst in ((0, St), (1, Ct)):
                D = lambda fn, r, w: P.op("dve", fn, reads=r, writes=w)
                if which == 0:
                    D(lambda e: e.tensor_copy(out=t_a[:], in_=ang[:]), ["ang"], ["t_a"])
                else:
                    D(lambda e: e.tensor_scalar(out=t_a[:], in0=ang[:], scalar1=math.pi / 2, scalar2=None, op0=ALU.add), ["ang"], ["t_a"])
                D(lambda e: e.tensor_scalar(out=t_k[:], in0=t_a[:], scalar1=1.0 / (2 * math.pi), scalar2=None, op0=ALU.mult), ["t_a"], ["t_k"])
                D(lambda e: e.tensor_copy(out=t_ki[:], in_=t_k[:]), ["t_k"], ["t_ki"])
                D(lambda e: e.tensor_copy(out=t_k[:], in_=t_ki[:]), ["t_ki"], ["t_k"])
                D(lambda e: e.scalar_tensor_tensor(out=t_a[:], in0=t_k[:], scalar=-C1, in1=t_a[:], op0=ALU.mult, op1=ALU.add), ["t_k", "t_a"], ["t_a"])
                D(lambda e: e.scalar_tensor_tensor(out=t_a[:], in0=t_k[:], scalar=-C2, in1=t_a[:], op0=ALU.mult, op1=ALU.add), ["t_k", "t_a"], ["t_a"])
                D(lambda e: e.tensor_scalar(out=t_k[:], in0=t_a[:], scalar1=math.pi, scalar2=2 * math.pi, op0=ALU.is_gt, op1=ALU.mult), ["t_a"], ["t_k"])
                D(lambda e: e.tensor_tensor(out=t_a[:], in0=t_a[:], in1=t_k[:], op=ALU.subtract), ["t_a", "t_k"], ["t_a"])
                D(lambda e: e.tensor_scalar(out=t_a[:], in0=t_a[:], scalar1=-math.pi, scalar2=math.pi, op0=ALU.max, op1=ALU.min), ["t_a"], ["t_a"])
                if which == 0:
                    P.op("act", lambda e, dst=dst: e.activation(out=dst[:], in_=t_a[:], func=AF.Sin, scale=csts[:, 1:2]),
                         reads=["t_a", "cst"], writes=["St"])
                else:
                    P.op("act", lambda e, dst=dst: e.activation(out=dst[:], in_=t_a[:], func=AF.Sin),
                         reads=["t_a"], writes=["Ct"])

        def load_hn2(i):
            for c in range(8):
                P.op("sp", lambda e, c=c: e.dma_start(out=hn2[:, c, :], in_=hn2T[c * 128:(c + 1) * 128, i * TQ:(i + 1) * TQ]),
                     writes=[("hn2", c)], dma=f"hn2_{c}")

        def mm(bank, M, parts, reads, ncols=512):
            def f(e):
                ins = None
                for n, (l, r) in enumerate(parts):
                    ins = e.matmul(ps[bank][0:M, 0:ncols], lhsT=l, rhs=r, start=(n == 0), stop=(n == len(parts) - 1))
                return ins
            P.op("pe", f, reads=reads, writes=[("ps", bank)])

        def rope_combine(bA, bB, lo, hi, out_ap, out_res):
            s = st["tmp"]
            st["tmp"] = (s + 1) % 2
            P.op("dve", lambda e: e.tensor_tensor(out=tmp1[lo:hi, s, :], in0=ps[bA][lo:hi, :], in1=Ct[lo:hi, :], op=ALU.mult),
                 reads=[("ps", bA), "Ct"], writes=[("tmp1", s)])
            P.op("dve", lambda e: e.tensor_tensor(out=tmp2[lo:hi, s, :], in0=ps[bB][lo:hi, :], in1=St[lo:hi, :], op=ALU.mult),
                 reads=[("ps", bB), "St"], writes=[("tmp2", s)])
            P.op("pool", lambda e: e.tensor_tensor(out=out_ap, in0=tmp1[lo:hi, s, :], in1=tmp2[lo:hi, s, :], op=ALU.add),
                 reads=[("tmp1", s), ("tmp2", s)], writes=out_res)

        hn2_res = [("hn2", c) for c in range(8)]

        def wv_(name, shape_str, **kw):
            return W[name][:].rearrange(shape_str, **kw)

        def proj_mla(i):
            cs = slice(i * TQ, (i + 1) * TQ)
            wql = wv_("wq_lat", "p (k n) -> p k n", n=256)
            wkvl = wv_("wkv_lat", "p (k n) -> p k n", n=128)
            wkr = wv_("wkr", "p (s k n) -> p s k n", s=2, n=96)
            wqu = wv_("wq_up", "p (h s k n) -> p h s k n", h=4, s=2, n=96)
            wku = wv_("wk_up", "p (h n) -> p h n", n=96)
            wvv = W["wv"]
            for oc in range(2):
                b = nb()
                mm(b, 128, [(wql[:, kc, oc * 128:(oc + 1) * 128], hn2[:, kc, :]) for kc in range(8)], hn2_res + [("W", "wq_lat")])
                P.op("dve", lambda e, b=b, oc=oc: e.tensor_copy(out=zq[:, oc, :], in_=ps[b][:]), reads=[("ps", b)], writes=[("zq", oc)])
                P.op("act", lambda e, oc=oc: e.activation(out=sqb[:, oc, :], in_=zq[:, oc, :], func=AF.Square),
                     reads=[("zq", oc)], writes=[("sqb", oc)])
                P.op("pe", lambda e, oc=oc: e.matmul(ps[7][:], lhsT=ones[:], rhs=sqb[:, oc, :], start=(oc == 0), stop=(oc == 1)),
                     reads=[("sqb", oc), "ones"], writes=[("ps", 7)])
            P.op("act", lambda e: e.activation(out=rstd[:, 0, :], in_=ps[7][:], func=AF.Sqrt, scale=1.0 / 256, bias=EPS_A),
                 reads=[("ps", 7)], writes=[("rstd", 0)])
            P.op("dve", lambda e: e.reciprocal(out=rstd[:, 0, :], in_=rstd[:, 0, :]), reads=[("rstd", 0)], writes=[("rstd", 0)])
            for oc in range(2):
                P.op("dve", lambda e, oc=oc: e.scalar_tensor_tensor(out=qn[:, oc, :], in0=zq[:, oc, :], scalar=gn[:, oc:oc + 1],
                                                                    in1=rstd[:, 0, :], op0=ALU.mult, op1=ALU.mult),
                     reads=[("zq", oc), ("rstd", 0), "gn"], writes=[("qn", oc)])
            b = nb()
            mm(b, 128, [(wkvl[:, kc, :], hn2[:, kc, :]) for kc in range(8)], hn2_res + [("W", "wkv_lat")])
            P.op("dve", lambda e, b=b: e.tensor_copy(out=zkv[:], in_=ps[b][:]), reads=[("ps", b)], writes=["zkv"])
            P.op("act", lambda e: e.activation(out=sqb[:, 0, :], in_=zkv[:], func=AF.Square), reads=["zkv"], writes=[("sqb", 0)])
            P.op("pe", lambda e: e.matmul(ps[7][:], lhsT=ones[:], rhs=sqb[:, 0, :], start=True, stop=True),
                 reads=[("sqb", 0), "ones"], writes=[("ps", 7)])
            P.op("act", lambda e: e.activation(out=rstd[:, 1, :], in_=ps[7][:], func=AF.Sqrt, scale=1.0 / 128, bias=EPS_A),
                 reads=[("ps", 7)], writes=[("rstd", 1)])
            P.op("dve", lambda e: e.reciprocal(out=rstd[:, 1, :], in_=rstd[:, 1, :]), reads=[("rstd", 1)], writes=[("rstd", 1)])
            P.op("dve", lambda e: e.scalar_tensor_tensor(out=kvn[:], in0=zkv[:], scalar=gn[:, 2:3], in1=rstd[:, 1, :],
                                                         op0=ALU.mult, op1=ALU.mult),
                 reads=["zkv", ("rstd", 1), "gn"], writes=["kvn"])
            bA, bB = nb(), nb()
            mm(bA, 96, [(wkr[:, 0, kc, :], hn2[:, kc, :]) for kc in range(8)], hn2_res + [("W", "wkr")])
            mm(bB, 96, [(wkr[:, 1, kc, :], hn2[:, kc, :]) for kc in range(8)], hn2_res + [("W", "wkr")])
            rope_combine(bA, bB, 64, 96, K[64:96, 0, cs], [("Kr", 0, i)])
            for h in range(1, 4):
                P.op("pool", lambda e, h=h: e.tensor_copy(out=K[64:96, h, cs], in_=K[64:96, 0, cs]),
                     reads=[("Kr", 0, i)], writes=[("Kr", h, i)])
            for h in range(4):
                b = nb()
                mm(b, 96, [(wku[:, h, :], kvn[:])], ["kvn", ("W", "wk_up")])
                P.op("act", lambda e, b=b, h=h: e.activation(out=K[0:64, h, cs], in_=ps[b][0:64, :], func=AF.Copy),
                     reads=[("ps", b)], writes=[("Kn", h, i)])
            for blk in range(4):
                b = nb()
                mm(b, 128, [(kvn[:, blk * 128:(blk + 1) * 128], wvv[:, :])], ["kvn", ("W", "wv")], ncols=256)
                bi_ = 4 * i + blk
                P.op("act", lambda e, b=b, bi_=bi_: e.activation(
                    out=V[:, bi_, :, 0:64], in_=ps[b][:, 0:256].rearrange("p (h d) -> p h d", d=64), func=AF.Copy),
                    reads=[("ps", b)], writes=[("V", bi_)])
            for h in range(4):
                bA, bB = nb(), nb()
                mm(bA, 96, [(wqu[:, h, 0, kc, :], qn[:, kc, :]) for kc in range(2)], [("qn", 0), ("qn", 1), ("W", "wq_up")])
                mm(bB, 96, [(wqu[:, h, 1, kc, :], qn[:, kc, :]) for kc in range(2)], [("qn", 0), ("qn", 1), ("W", "wq_up")])
                P.op("dve", lambda e, bA=bA, h=h: e.tensor_copy(out=Q[0:64, h, :], in_=ps[bA][0:64, :]),
                     reads=[("ps", bA)], writes=[("Qn", h)])
                rope_combine(bA, bB, 64, 96, Q[64:96, h, :], [("Qr", h)])

        def proj_dil(i):
            cs = slice(i * TQ, (i + 1) * TQ)
            wq = wv_("wqd", "p (s a k n) -> p s a k n", s=2, a=2, n=128)
            wk = wv_("wkd", "p (s a k n) -> p s a k n", s=2, a=2, n=128)
            wvv = wv_("wvd", "p (k n) -> p k n", n=256)
            for a in range(2):
                bA, bB = nb(), nb()
                mm(bA, 128, [(wq[:, 0, a, kc, :], hn2[:, kc, :]) for kc in range(8)], hn2_res + [("W", "wqd")])
                mm(bB, 128, [(wq[:, 1, a, kc, :], hn2[:, kc, :]) for kc in range(8)], hn2_res + [("W", "wqd")])
                rope_combine(bA, bB, 0, 128, Q[:, a, :], [("Q", a)])
                bA, bB = nb(), nb()
                mm(bA, 128, [(wk[:, 0, a, kc, :], hn2[:, kc, :]) for kc in range(8)], hn2_res + [("W", "wkd")])
                mm(bB, 128, [(wk[:, 1, a, kc, :], hn2[:, kc, :]) for kc in range(8)], hn2_res + [("W", "wkd")])
                rope_combine(bA, bB, 0, 128, K[:, a, cs], [("K", a, i)])
            for blk in range(4):
                b = nb()
                mm(b, 128, [(hn2[:, kc, blk * 128:(blk + 1) * 128], wvv[:, kc, :]) for kc in range(8)],
                   hn2_res + [("W", "wvd")], ncols=256)
                bi_ = 4 * i + blk
                P.op("act", lambda e, b=b, bi_=bi_: e.activation(
                    out=V[:, bi_, :, 0:64], in_=ps[b][:, 0:256].rearrange("p (h d) -> p h d", d=64), func=AF.Copy),
                    reads=[("ps", b)], writes=[("V", bi_)])

        def finalize(i, h):
            ob = 5 + (h % 2)
            P.op("dve", lambda e: e.reciprocal(out=rl[64:65, :], in_=ps[ob][64:65, :]), reads=[("ps", ob)], writes=["rl"])
            P.op("dve", lambda e: e.tensor_copy(out=rh[64:65, :], in_=rl[64:65, :]), reads=["rl"], writes=["rh"])
            P.op("dve", lambda e: e.tensor_tensor(out=rlo[64:65, :], in0=rl[64:65, :], in1=rh[64:65, :], op=ALU.subtract),
                 reads=["rl", "rh"], writes=["rlo"])

            def bc(e):
                e.matmul(ps[7][0:64, :], lhsT=ones[64:65, 0:64], rhs=rh[64:65, :], start=True, stop=False)
                return e.matmul(ps[7][0:64, :], lhsT=ones[64:65, 0:64], rhs=rlo[64:65, :], start=False, stop=True)
            P.op("pe", bc, reads=["rh", "rlo", "ones"], writes=[("ps", 7)])
            P.op("act", lambda e: e.activation(out=rlb[0:64, :], in_=ps[7][0:64, :], func=AF.Copy), reads=[("ps", 7)], writes=["rlb"])
            os_ = st["os"]
            st["os"] = (os_ + 1) % 2
            P.op("dve", lambda e: e.tensor_tensor(out=osb[0:64, os_, :], in0=ps[ob][0:64, :], in1=rlb[0:64, :], op=ALU.mult),
                 reads=[("ps", ob), "rlb"], writes=[("osb", os_)])
            finals.append(P.op("sp", lambda e: e.dma_start(out=oT[h * 64:(h + 1) * 64, i * TQ:(i + 1) * TQ], in_=osb[0:64, os_, :]),
                               reads=[("osb", os_)], dma=f"o{os_}"))

        def attend_tile(i):
            LA = 2
            visits = []
            for h in range(4):
                kbs = list(range(0, 4 * i + 4)) if mla else list(range(max(0, 4 * i - 16), 4 * i + 4))
                for n, kb in enumerate(kbs):
                    visits.append((h, n, kb, len(kbs)))
            scale = (96.0 if mla else 64.0) ** -0.5
            info = {}

            def issue_qk(v):
                h, n, kb, nk = visits[v]
                sbk = 2 + st["sbank"]
                st["sbank"] = (st["sbank"] + 1) % 3
                pt = st["pt"]
                st["pt"] = (pt + 1) % 6
                info[v] = pt
                kt = kb // 4
                if mla:
                    lhsT = K[0:96, h, kb * 128:(kb + 1) * 128]
                    rhs = Q[0:96, h, :]
                    kres = [("Kn", h, kt), ("Kr", h, kt)]
                    qres = [("Qn", h), ("Qr", h)]
                    midx = (4 * i - kb) + 3 if kb >= 4 * i else None
                else:
                    a, base = h // 2, (h % 2) * 64
                    lhsT = K[base:base + 64, a, kb * 128:(kb + 1) * 128]
                    rhs = Q[base:base + 64, a, :]
                    kres = [("K", a, kt)]
                    qres = [("Q", a)]
                    midx = (4 * i - kb) + 3
                P.op("pe", lambda e: e.matmul(ps[sbk][:], lhsT=lhsT, rhs=rhs, start=True, stop=True),
                     reads=kres + qres, writes=[("ps", sbk)])
                P.op("act", lambda e: e.activation(out=PT[:, pt, :], in_=ps[sbk][:], func=AF.Exp, scale=scale),
                     reads=[("ps", sbk)], writes=[("PT", pt)])
                if midx is not None:
                    P.op("dve", lambda e: e.tensor_tensor(out=PT[:, pt, :], in0=PT[:, pt, :], in1=masks[:, midx, :], op=ALU.mult),
                         reads=[("PT", pt), "mask"], writes=[("PT", pt)])

            def issue_pv(v):
                h, n, kb, nk = visits[v]
                pt = info[v]
                ob = 5 + (h % 2)
                P.op("pe", lambda e: e.matmul(ps[ob][0:65, :], lhsT=V[:, kb, h, 0:65], rhs=PT[:, pt, :],
                                              start=(n == 0), stop=(n == nk - 1)),
                     reads=[("V", kb), ("PT", pt)], writes=[("ps", ob)])
                if n == nk - 1:
                    finalize(i, h)

            for v in range(len(visits) + LA):
                if v < len(visits):
                    issue_qk(v)
                if v - LA >= 0:
                    issue_pv(v - LA)

        for i in range(NT):
            load_hn2(i)
            tables(i)
            if mla:
                proj_mla(i)
            else:
                proj_dil(i)
            attend_tile(i)
        P.emit(finals, max_ops=max_ops)
    return nc


from concourse.bass_utils import run_bass_kernel_spmd

_PROGS = {}


def _prog(key, fn):
    if key not in _PROGS:
        _PROGS[key] = fn()
    return _PROGS[key]


def _run(nc, in_maps):
    res = run_bass_kernel_spmd(nc, in_maps, core_ids=list(range(8)))
    return res.results


def kernel(x, p, positions, norm_gains, w_in, q_norm, w_q_up, kv_norm, w_kv_up,
           group_out_norm, w_out, ffn_gate, ffn_up, ffn_down, w_ple, w_ple_gate):
    f32 = lambda a: np.ascontiguousarray(np.asarray(a, dtype=np.float32))
    x, p, norm_gains, w_in, q_norm, w_q_up, kv_norm, w_kv_up = map(f32, (x, p, norm_gains, w_in, q_norm, w_q_up, kv_norm, w_kv_up))
    group_out_norm, w_out, ffn_gate, ffn_up, ffn_down, w_ple, w_ple_gate = map(
        f32, (group_out_norm, w_out, ffn_gate, ffn_up, ffn_down, w_ple, w_ple_gate))
    positions = np.ascontiguousarray(np.asarray(positions, dtype=np.int32))
    B, S, D = x.shape
    L = norm_gains.shape[0]
    cores = [(b, j) for b in range(B) for j in range(2)]

    def gains_pack(lp, ln):
        g = np.zeros((18, D), np.float32)
        if lp is not None:
            g[0:8] = norm_gains[lp]
            g[8] = group_out_norm[lp]
        if ln is not None:
            g[9:17] = norm_gains[ln]
            g[17] = group_out_norm[ln]
        return prep_vec(g)

    def ffn_w(l, s, sfx):
        return {"wg" + sfx: prep_w(ffn_gate[l, s]), "wu" + sfx: prep_w(ffn_up[l, s]), "wd" + sfx: prep_w(ffn_down[l, s])}

    def p4_w(l):
        d = {"w_out": prep_w(w_out[l]), "w_pg": prep_w(w_ple_gate[l]), "w_pl": prep_w(w_ple[l])}
        d.update(ffn_w(l, 1, "b"))
        return d

    mc, mm_ = mla_consts()
    dc, dm_ = dil_consts()
    mm_ = np.ascontiguousarray(mm_.reshape(128, -1))
    dm_ = np.ascontiguousarray(dm_.reshape(128, -1))

    def attention(l, hn2_shards):
        full = [np.ascontiguousarray(np.concatenate([hn2_shards[2 * b], hn2_shards[2 * b + 1]], axis=1)) for b in range(B)]
        mw = [prep_mla_weights(w_in[l], w_q_up[l], w_kv_up[l], q_norm[l], kv_norm[l], j) for j in range(2)]
        dw = [prep_dil_weights(w_in[l], j) for j in range(2)]
        ims_m, ims_d = [], []
        for (b, j) in cores:
            base = {"hn2T": full[b], "pos": positions[b][None, :]}
            ims_m.append(dict(base, cst=mc, mask=mm_, **mw[j]))
            ims_d.append(dict(base, cst=dc, mask=dm_, **dw[j]))
        om = _run(_prog("Am", lambda: build_A("mla", S)), ims_m)
        od = _run(_prog("Ad", lambda: build_A("dil", S)), ims_d)
        outs = []
        for b in range(B):
            o_full = np.concatenate([om[2 * b]["oT"], om[2 * b + 1]["oT"], od[2 * b]["oT"], od[2 * b + 1]["oT"]], axis=0)
            for j in range(2):
                outs.append(np.ascontiguousarray(o_full[:, j * NTOK:(j + 1) * NTOK]))
        return outs

    hT = [np.ascontiguousarray(x[b, j * NTOK:(j + 1) * NTOK].T) for (b, j) in cores]
    g = gains_pack(None, 0)
    wa = ffn_w(0, 0, "a")
    r = _run(_prog("R0", lambda: build_R(False, True)), [dict(hT=hT[c], gains=g, **wa) for c in range(8)])
    hT = [r[c]["hT_out"] for c in range(8)]
    hn2 = [r[c]["hn2T"] for c in range(8)]
    for l in range(L):
        oT = attention(l, hn2)
        last = l == L - 1
        wp = p4_w(l)
        pT = [np.ascontiguousarray(p[l, b, j * NTOK:(j + 1) * NTOK].T.reshape(2, 128, NTOK // TW, TW).transpose(2, 0, 1, 3)) for (b, j) in cores]
        if last:
            g = gains_pack(l, None)
            r = _run(_prog("R2", lambda: build_R(True, False)),
                     [dict(hT=hT[c], gains=g, oT=oT[c], pT=pT[c], **wp) for c in range(8)])
        else:
            g = gains_pack(l, l + 1)
            wa = ffn_w(l + 1, 0, "a")
            r = _run(_prog("R1", lambda: build_R(True, True)),
                     [dict(hT=hT[c], gains=g, oT=oT[c], pT=pT[c], **wp, **wa) for c in range(8)])
            hn2 = [r[c]["hn2T"] for c in range(8)]
        hT = [r[c]["hT_out"] for c in range(8)]
    out = np.empty((B, S, D), np.float32)
    for c, (b, j) in enumerate(cores):
        out[b, j * NTOK:(j + 1) * NTOK, :] = np.asarray(hT[c], np.float32).T
    return out
```
